# Optimizing a Trainium2 kernel written in Bass

```python
import jax, jax.numpy as jnp
from jax import lax
import numpy as np

D_MODEL = 1024
BATCH = 16
SEQ = 2048
DEPTH = 2
DEC_BATCH = 1
DEC_SEQ = 16384
PAST_LEN = 128

N_MIXERS = 2
D_FF = 2816
PLE_DIM = 256
NORM_EPS = 1e-6
RW_HEAD = 64
RW_HEADS = D_MODEL // RW_HEAD
RW_DECAY_LORA = 64
RW_ICLR_LORA = 64
RW_GATE_LORA = 128
RW_LNX_EPS = 64e-5
NA_HEAD = 32
NA_HEADS = D_MODEL // NA_HEAD
GRID_W = 64
NA_KH_MAX = 8
NA_KW = 16
NA_QB = 16
NA_KVB = NA_QB + NA_KW
N_RWKV = (DEPTH + 1) // 2
N_NA = DEPTH // 2

kernel_name = 'hybrid_rwkv7_natten_macaron_encoder'


def rms_norm(x, gain):
    xf = x.astype(jnp.float32)
    y = xf * lax.rsqrt(jnp.mean(xf * xf, axis=-1, keepdims=True) + NORM_EPS)
    return (y * gain.astype(jnp.float32)).astype(x.dtype)


def swiglu(x, w_in, w_out):
    gate, up = jnp.split(x @ w_in, 2, axis=-1)
    return (jax.nn.silu(gate) * up) @ w_out


def _heads(t):
    return t.reshape(t.shape[:-1] + (RW_HEADS, RW_HEAD)).astype(jnp.float32)


def rwkv7_bidir_scan(r, w, k, v, kk, a):
    def step(s, inp):
        r_t, w_t, k_t, v_t, kk_t, a_t = inp
        s_kk = jnp.einsum('zbhvk,zbhk->zbhv', s, kk_t)
        s = (s * w_t[..., None, :]
             - s_kk[..., :, None] * (kk_t * a_t)[..., None, :]
             + v_t[..., :, None] * k_t[..., None, :])
        return s, jnp.einsum('zbhvk,zbhk->zbhv', s, r_t)
    xs = tuple(jnp.moveaxis(t, 2, 0) for t in (r, w, k, v, kk, a))
    z, b, _, h, n = r.shape
    s0 = jnp.zeros((z, b, h, n, n), jnp.float32)
    _, y = lax.scan(step, s0, xs)
    return jnp.moveaxis(y, 0, 2)


def rwkv7_time_mix(x, mu, w_rkv, w0, w1, w2, a0, a1, a2, g1, g2, k_k, k_a, r_k, lnx_w, lnx_b, w_o):
    bsz, seq, d = x.shape
    f32 = jnp.float32
    prev = jnp.pad(x[:, :-1], ((0, 0), (1, 0), (0, 0)))
    nxt = jnp.pad(x[:, 1:], ((0, 0), (0, 1), (0, 0)))
    xm = x[None] + (0.5 * (prev + nxt) - x)[None] * mu[:, None, None, :]
    r, k, v = jnp.einsum('nbtd,nde->nbte', xm[:3], w_rkv)
    xw, xa, xg = xm[3], xm[4], xm[5]
    w_lora = jnp.einsum('zbtr,zrd->zbtd', jnp.tanh(jnp.einsum('btd,zdr->zbtr', xw, w1)), w2)
    log_w = -jax.nn.softplus(-(w0[:, None, None, :] + w_lora).astype(f32)) - 0.5
    decay = jnp.exp(-jnp.exp(log_w))
    a = jax.nn.sigmoid((a0[:, None, None, :]
                        + jnp.einsum('zbtr,zrd->zbtd', jnp.einsum('btd,zdr->zbtr', xa, a1), a2)).astype(f32))
    g = jax.nn.sigmoid(xg @ g1) @ g2
    kd = k.astype(f32)[None] * (1.0 + (a - 1.0) * k_a.astype(f32))
    kk = _heads(k * k_k)
    kk = kk * lax.rsqrt(jnp.maximum(jnp.sum(kk * kk, axis=-1, keepdims=True), 1e-24))
    rh, vh, kdh, ah, wh = _heads(r), _heads(v), _heads(kd), _heads(a), _heads(decay)
    both = lambda t: jnp.stack([t, jnp.flip(t, axis=1)])
    per_dir = lambda t: jnp.stack([t[0], jnp.flip(t[1], axis=1)])
    y = rwkv7_bidir_scan(both(rh), per_dir(wh), per_dir(kdh), both(vh), both(kk), per_dir(ah))
    y = y[0] + jnp.flip(y[1], axis=1)
    mean = jnp.mean(y, axis=-1, keepdims=True)
    var = jnp.mean(jnp.square(y - mean), axis=-1, keepdims=True)
    yn = (y - mean) * lax.rsqrt(var + RW_LNX_EPS)
    yn = yn * _heads(lnx_w) + _heads(lnx_b)
    bonus = jnp.sum(rh[None] * kdh * r_k.astype(f32), axis=(0, -1))[..., None] * vh
    out = (yn + bonus).reshape(bsz, seq, d).astype(x.dtype) * g
    return out @ w_o


def _na_column_layout():
    ncb = GRID_W // NA_QB
    q_cols = np.arange(ncb)[:, None] * NA_QB + np.arange(NA_QB)[None, :]
    kv_start = np.clip(np.arange(ncb) * NA_QB - NA_KW // 2, 0, GRID_W - NA_KVB)
    kv_cols = kv_start[:, None] + np.arange(NA_KVB)[None, :]
    win_start = np.clip(q_cols - NA_KW // 2, 0, GRID_W - NA_KW)
    kc = kv_cols[:, None, :]
    col_mask = (kc >= win_start[:, :, None]) & (kc < win_start[:, :, None] + NA_KW)
    col_idx = np.clip(kc - q_cols[:, :, None], -(NA_KW - 1), NA_KW - 1) + NA_KW - 1
    return kv_cols, col_mask, col_idx


def neighbourhood_attention(x, w_qkv, b_qkv, rpb, w_o, b_o):
    bsz, seq, d = x.shape
    rows = seq // GRID_W
    kh = min(NA_KH_MAX, rows)
    ncb = GRID_W // NA_QB
    kv_cols, col_mask, col_idx = _na_column_layout()
    q, k, v = jnp.split(x @ w_qkv + b_qkv, 3, axis=-1)
    grid = lambda t: t.reshape(bsz, rows, GRID_W, NA_HEADS, NA_HEAD)
    q, k, v = grid(q) * (NA_HEAD ** -0.5), grid(k), grid(v)
    rpb_c = rpb[:, :, col_idx]
    mask = col_mask[:, :, None, :]

    def row_block(i):
        rs = jnp.clip(i - kh // 2, 0, rows - kh)
        q_i = lax.dynamic_index_in_dim(q, i, axis=1, keepdims=False).reshape(bsz, ncb, NA_QB, NA_HEADS, NA_HEAD)
        k_i = lax.dynamic_slice_in_dim(k, rs, kh, axis=1)[:, :, kv_cols]
        v_i = lax.dynamic_slice_in_dim(v, rs, kh, axis=1)[:, :, kv_cols]
        s = jnp.einsum('bcqhd,brckhd->bhcqrk', q_i, k_i).astype(jnp.float32)
        row_idx = rs + jnp.arange(kh) - i + NA_KH_MAX - 1
        bias = jnp.transpose(rpb_c[:, row_idx], (0, 2, 3, 1, 4)).astype(jnp.float32)
        s = jnp.where(mask, s + bias, -1e30)
        pr = jax.nn.softmax(s.reshape(s.shape[:4] + (kh * NA_KVB,)), axis=-1).reshape(s.shape)
        o = jnp.einsum('bhcqrk,brckhd->bcqhd', pr.astype(v.dtype), v_i)
        return o.reshape(bsz, GRID_W, d)

    o = lax.map(row_block, jnp.arange(rows))
    o = jnp.moveaxis(o, 0, 1).reshape(bsz, seq, d)
    return o @ w_o + b_o


def encoder_trunk(x, p, prm):
    h = x
    for i in range(DEPTH):
        j = i // N_MIXERS
        h = h + 0.5 * swiglu(rms_norm(h, prm['ffn_norm'][i, 0]), prm['ffn_w_in'][i, 0], prm['ffn_w_out'][i, 0])
        hn = rms_norm(h, prm['mix_norm'][i])
        if i % N_MIXERS == 0:
            h = h + rwkv7_time_mix(hn, prm['rw_mu'][j], prm['rw_w_rkv'][j], prm['rw_w0'][j], prm['rw_w1'][j],
                                   prm['rw_w2'][j], prm['rw_a0'][j], prm['rw_a1'][j], prm['rw_a2'][j],
                                   prm['rw_g1'][j], prm['rw_g2'][j], prm['rw_k_k'][j], prm['rw_k_a'][j],
                                   prm['rw_r_k'][j], prm['rw_lnx_w'][j], prm['rw_lnx_b'][j], prm['rw_w_o'][j])
        else:
            h = h + neighbourhood_attention(hn, prm['na_w_qkv'][j], prm['na_b_qkv'][j], prm['na_rpb'][j],
                                            prm['na_w_o'][j], prm['na_b_o'][j])
        h = h + 0.5 * swiglu(rms_norm(h, prm['ffn_norm'][i, 1]), prm['ffn_w_in'][i, 1], prm['ffn_w_out'][i, 1])
        gate = jax.nn.sigmoid(rms_norm(h, prm['ple_norm'][i]) @ prm['ple_w_gate'][i])
        h = h + gate * (p[i] @ prm['ple_w_proj'][i])
    return rms_norm(h, prm['final_norm'])


def setup_inputs(seed: int = 0) -> dict:
    key = jax.random.key(seed)
    ks = iter(jax.random.split(key, 48))
    f32 = jnp.float32
    nrm = lambda shape, scale: jax.random.normal(next(ks), shape, f32) * scale
    uni = lambda shape, lo, hi: jax.random.uniform(next(ks), shape, f32, lo, hi)
    D = D_MODEL
    return {
        'x_prompt': nrm((BATCH, SEQ, D), 1.0),
        'x_sample': nrm((DEC_BATCH, DEC_SEQ, D), 1.0),
        'p_prompt': nrm((DEPTH, BATCH, SEQ, PLE_DIM), 1.0),
        'p_sample': nrm((DEPTH, DEC_BATCH, DEC_SEQ, PLE_DIM), 1.0),
        'ffn_norm': 1.0 + nrm((DEPTH, 2, D), 0.02),
        'ffn_w_in': nrm((DEPTH, 2, D, 2 * D_FF), D ** -0.5),
        'ffn_w_out': nrm((DEPTH, 2, D_FF, D), D_FF ** -0.5),
        'mix_norm': 1.0 + nrm((DEPTH, D), 0.02),
        'ple_norm': 1.0 + nrm((DEPTH, D), 0.02),
        'ple_w_gate': nrm((DEPTH, D, D), D ** -0.5),
        'ple_w_proj': nrm((DEPTH, PLE_DIM, D), PLE_DIM ** -0.5),
        'final_norm': 1.0 + nrm((D,), 0.02),
        'rw_mu': uni((N_RWKV, 6, D), 0.0, 1.0),
        'rw_w_rkv': nrm((N_RWKV, 3, D, D), D ** -0.5),
        'rw_w0': uni((N_RWKV, 2, D), -6.0, 1.0),
        'rw_w1': nrm((N_RWKV, 2, D, RW_DECAY_LORA), D ** -0.5),
        'rw_w2': nrm((N_RWKV, 2, RW_DECAY_LORA, D), 0.5 * RW_DECAY_LORA ** -0.5),
        'rw_a0': nrm((N_RWKV, 2, D), 0.5),
        'rw_a1': nrm((N_RWKV, 2, D, RW_ICLR_LORA), D ** -0.5),
        'rw_a2': nrm((N_RWKV, 2, RW_ICLR_LORA, D), 0.5 * RW_ICLR_LORA ** -0.5),
        'rw_g1': nrm((N_RWKV, D, RW_GATE_LORA), D ** -0.5),
        'rw_g2': nrm((N_RWKV, RW_GATE_LORA, D), RW_GATE_LORA ** -0.5),
        'rw_k_k': 1.0 + nrm((N_RWKV, D), 0.02),
        'rw_k_a': 1.0 + nrm((N_RWKV, D), 0.02),
        'rw_r_k': nrm((N_RWKV, RW_HEADS, RW_HEAD), 0.1),
        'rw_lnx_w': 1.0 + nrm((N_RWKV, D), 0.02),
        'rw_lnx_b': nrm((N_RWKV, D), 0.02),
        'rw_w_o': nrm((N_RWKV, D, D), D ** -0.5),
        'na_w_qkv': nrm((N_NA, D, 3 * D), D ** -0.5),
        'na_b_qkv': nrm((N_NA, 3 * D), 0.02),
        'na_rpb': nrm((N_NA, NA_HEADS, 2 * NA_KH_MAX - 1, 2 * NA_KW - 1), 0.1),
        'na_w_o': nrm((N_NA, D, D), D ** -0.5),
        'na_b_o': nrm((N_NA, D), 0.02),
    }


def reference(x_prompt, x_sample, p_prompt, p_sample, ffn_norm, ffn_w_in, ffn_w_out, mix_norm, ple_norm,
              ple_w_gate, ple_w_proj, final_norm, rw_mu, rw_w_rkv, rw_w0, rw_w1, rw_w2, rw_a0, rw_a1, rw_a2,
              rw_g1, rw_g2, rw_k_k, rw_k_a, rw_r_k, rw_lnx_w, rw_lnx_b, rw_w_o, na_w_qkv, na_b_qkv, na_rpb,
              na_w_o, na_b_o):
    prm = dict(ffn_norm=ffn_norm, ffn_w_in=ffn_w_in, ffn_w_out=ffn_w_out, mix_norm=mix_norm,
               ple_norm=ple_norm, ple_w_gate=ple_w_gate, ple_w_proj=ple_w_proj, final_norm=final_norm,
               rw_mu=rw_mu, rw_w_rkv=rw_w_rkv, rw_w0=rw_w0, rw_w1=rw_w1, rw_w2=rw_w2, rw_a0=rw_a0,
               rw_a1=rw_a1, rw_a2=rw_a2, rw_g1=rw_g1, rw_g2=rw_g2, rw_k_k=rw_k_k, rw_k_a=rw_k_a,
               rw_r_k=rw_r_k, rw_lnx_w=rw_lnx_w, rw_lnx_b=rw_lnx_b, rw_w_o=rw_w_o, na_w_qkv=na_w_qkv,
               na_b_qkv=na_b_qkv, na_rpb=na_rpb, na_w_o=na_w_o, na_b_o=na_b_o)
    y_prompt = encoder_trunk(x_prompt, p_prompt, prm)
    y_sample = encoder_trunk(x_sample, p_sample, prm)
    return (y_prompt, y_sample)
```

```python
import numpy as np
from contextlib import ExitStack
import concourse.bass as bass
import concourse.mybir as mybir
from concourse.bass_utils import run_bass_kernel_spmd

F32 = mybir.dt.float32
BF16 = mybir.dt.bfloat16
AF = mybir.ActivationFunctionType
ALU = mybir.AluOpType

D = 1024
DFF = 2816
NJ = DFF // 128
PLE = 256
DEPTH = 2
EPS = 1e-6
NCORES = 8


class Tok:
    __slots__ = ("w", "r", "name")

    def __init__(self, name=""):
        self.w = None
        self.r = {}
        self.name = name


class Op:
    __slots__ = ("eng", "fn", "deps", "needs_inc", "seq", "dma", "dsem", "dval")

    def __init__(self, eng, fn, dma):
        self.eng, self.fn, self.dma = eng, fn, dma
        self.deps = []
        self.needs_inc = False
        self.seq = None
        self.dsem = None
        self.dval = None


ENGS = ("pe", "act", "dve", "pool", "sp")
NDSEM = 8


class Prog:
    def __init__(self, nc):
        self.nc = nc
        self.ops = {e: [] for e in ENGS}

    def op(self, eng, fn, reads=(), writes=(), dma=False):
        o = Op(eng, fn, dma)
        deps = {}
        for t in reads:
            if t.w is not None:
                deps[id(t.w)] = t.w
        for t in writes:
            if t.w is not None:
                deps[id(t.w)] = t.w
            for r in t.r.values():
                deps[id(r)] = r
        for d in deps.values():
            if d is o:
                continue
            if d.eng == "pe" and eng == "pe" and not d.dma:
                continue
            o.deps.append(d)
            d.needs_inc = True
        for t in writes:
            t.w = o
            t.r = {}
        for t in reads:
            if t.w is not o:
                t.r[(eng, id(o)) if dma else (eng, 0)] = o
        self.ops[eng].append(o)
        return o

    def barrier(self):
        lasts = []
        for e in ENGS:
            nd = 0
            got_c = False
            for o in reversed(self.ops[e]):
                if o.dma:
                    if nd < NDSEM:
                        lasts.append(o)
                        nd += 1
                elif not got_c:
                    lasts.append(o)
                    got_c = True
                if got_c and nd >= NDSEM:
                    break
        for e in ENGS:
            o = Op(e, lambda eng: eng.nop(), False)
            for d in lasts:
                if d.eng == e == "pe" and not d.dma:
                    continue
                o.deps.append(d)
                d.needs_inc = True
            self.ops[e].append(o)

    def emit(self, final_waits=True):
        nc = self.nc
        with ExitStack() as es:
            esem = {e: es.enter_context(nc.semaphore("sem_" + e)) for e in ENGS}
            dsem = {e: [es.enter_context(nc.semaphore("dsem_%s%d" % (e, i))) for i in range(NDSEM)]
                    for e in ("sp", "pool", "act")}
            for e in ENGS:
                n = 0
                nd = 0
                for o in self.ops[e]:
                    if o.dma:
                        o.dsem = dsem[e][nd % NDSEM]
                        o.dval = 16 * (nd // NDSEM + 1)
                        nd += 1
                    elif o.needs_inc:
                        n += 1
                        o.seq = n
            block = es.enter_context(nc.Block())
            ops = self.ops

            def body(ename):
                def f(e):
                    waited = {}

                    def wait(sem, val):
                        k = id(sem)
                        if waited.get(k, 0) >= val:
                            return
                        waited[k] = val
                        e.wait_ge(sem, val)

                    for o in ops[ename]:
                        for d in o.deps:
                            if d.dma:
                                wait(d.dsem, d.dval)
                            else:
                                wait(esem[d.eng], d.seq)
                        if o.dma:
                            if o.dval > 16:
                                wait(o.dsem, o.dval - 16)
                            o.fn(e).then_inc(o.dsem, 16)
                        else:
                            ins = o.fn(e)
                            if o.needs_inc:
                                ins.then_inc(esem[ename], 1)
                    if final_waits:
                        last = {}
                        for o in ops[ename]:
                            if o.dma:
                                last[id(o.dsem)] = (o.dsem, o.dval)
                        for sem, val in last.values():
                            wait(sem, val)
                return f

            if ops["pe"]:
                block.tensor(body("pe"))
            if ops["act"]:
                block.scalar(body("act"))
            if ops["dve"]:
                block.vector(body("dve"))
            if ops["pool"]:
                block.gpsimd(body("pool"))
            if ops["sp"]:
                block.sync(body("sp"))


class Ring:
    def __init__(self, bufs):
        self.bufs = bufs
        self.toks = [Tok() for _ in bufs]
        self.i = 0

    def next(self):
        k = self.i % len(self.bufs)
        self.i += 1
        return self.bufs[k], self.toks[k]


NA_HEADS = 32
NA_HD = 32
NHT = 11
ARENA_F32 = 35400
AX = mybir.AxisListType.X


def build(nseg, T, do_ffn=True, do_ple=True, do_final=True, do_na=True, do_rwkv=True, rw_stop=99, dbg=False, sample=False):
    nc = bass.Bass("TRN2", target_bir_lowering=False)
    NT = nseg * T
    TT = 512
    ntt = T // TT
    R = T // 64
    KH = min(8, R)

    def din(name, shape):
        return nc.dram_tensor(name, list(shape), F32, kind="ExternalInput")

    x_d = din("x", [NT, D])
    p_d = din("p", [DEPTH, NT, PLE])
    ident_d = din("ident", [128, 128])
    ffn_norm_d = din("ffn_norm", [DEPTH, 2, D])
    ffn_w_in_d = din("ffn_w_in", [DEPTH, 2, D, 2 * DFF])
    ffn_w_out_d = din("ffn_w_out", [DEPTH, 2, DFF, D])
    mix_norm_d = din("mix_norm", [DEPTH, D])
    ple_norm_d = din("ple_norm", [DEPTH, D])
    ple_w_gate_d = din("ple_w_gate", [DEPTH, D, D])
    ple_w_proj_d = din("ple_w_proj", [DEPTH, PLE, D])
    final_norm_d = din("final_norm", [D])
    na_w_qkv_d = din("na_w_qkv", [1, D, 3 * D])
    na_b_qkv_d = din("na_b_qkv", [1, 3 * D])
    na_w_o_d = din("na_w_o", [1, D, D])
    na_b_o_d = din("na_b_o", [1, D])
    btab_d = din("btab", [NA_HEADS, 64, 15 * 64])
    rw_mu_d = din("rw_mu", [1, 6, D])
    rw_w_rkv_d = din("rw_w_rkv", [1, 3, D, D])
    rw_w0_d = din("rw_w0", [1, 2, D])
    rw_w1_d = din("rw_w1", [1, 2, D, 64])
    rw_w2_d = din("rw_w2", [1, 2, 64, D])
    rw_a0_d = din("rw_a0", [1, 2, D])
    rw_a1_d = din("rw_a1", [1, 2, D, 64])
    rw_a2_d = din("rw_a2", [1, 2, 64, D])
    rw_g1_d = din("rw_g1", [1, D, 128])
    rw_g2_d = din("rw_g2", [1, 128, D])
    rw_k_k_d = din("rw_k_k", [1, D])
    rw_k_a_d = din("rw_k_a", [1, D])
    rw_r_k_d = din("rw_r_k", [1, 16, 64])
    rw_lnx_w_d = din("rw_lnx_w", [1, D])
    rw_lnx_b_d = din("rw_lnx_b", [1, D])
    rw_w_o_d = din("rw_w_o", [1, D, D])
    masks_d = din("masks", [2, 64, 256])
    bdones_d = din("bdones", [128, 128])
    y_d = nc.dram_tensor("y", [NT, D], F32, kind="ExternalOutput")
    xh_d = din("xh", [nseg, 2, D])
    RE = R + 8
    if sample:
        summ_all_d = din("summ_all", [8, 2, 2, 64, D])
        cmask_d = din("cmask", [64, 16])
        kvh_k_d = din("kvh_k", [2, D, 256])
        kvh_v_d = din("kvh_v", [2, 256, D])
        sel_d = din("sel", [64, 4])
        summ_o = nc.dram_tensor("summ_o", [2, 2, 64, D], F32, kind="ExternalOutput")
        kvo_k = nc.dram_tensor("kvo_k", [2, D, 256], F32, kind="ExternalOutput")
        kvo_v = nc.dram_tensor("kvo_v", [2, 256, D], F32, kind="ExternalOutput")
        K_all = nc.dram_tensor("K_all", [D, RE * 64], BF16)
        V_all = nc.dram_tensor("V_all", [RE * 64, D], BF16)
    if dbg:
        dbg_yb = nc.dram_tensor("dbg_yb", [128, 8, T], F32, kind="ExternalOutput")
        dbg_ys = nc.dram_tensor("dbg_ys", [128, 8, T], F32, kind="ExternalOutput")
        dbg_p = nc.dram_tensor("dbg_p", [128, 8, 256], F32, kind="ExternalOutput")

    win_s = [[nc.dram_tensor("win_s%d%d" % (i, h), [NJ, 128, 8, 256], BF16) for h in range(2)] for i in range(DEPTH)]
    wout_s = [[nc.dram_tensor("wout_s%d%d" % (i, h), [8, 128, NJ, 128], BF16) for h in range(2)] for i in range(DEPTH)]
    wg_s = [nc.dram_tensor("wg_s%d" % i, [8, 128, 8, 128], BF16) for i in range(DEPTH)]
    wp_s = [nc.dram_tensor("wp_s%d" % i, [8, 128, 2, 128], BF16) for i in range(DEPTH)]
    wqkv_s = [nc.dram_tensor("wqkv_s%d" % s_, [NHT, 128, 8, 96], BF16) for s_ in range(3)]
    wno_s = nc.dram_tensor("wno_s", [8, 128, 8, 128], BF16)
    wrkv_s = [nc.dram_tensor("wrkv_s%d" % s_, [8, 128, 8, 128], BF16) for s_ in range(3)]
    wro_s = nc.dram_tensor("wro_s", [8, 128, 8, 128], BF16)

    P = Prog(nc)
    es = ExitStack()

    def sb(name, shape, dt):
        return es.enter_context(nc.sbuf_tensor("s_" + name, list(shape), dt))

    H = sb("H", [128, 8, T], F32)
    H_tok = [[Tok("H%d_%d" % (c, t)) for t in range(ntt)] for c in range(8)]
    ident = sb("ident", [128, 128], F32)
    identb = sb("identb", [128, 128], BF16)
    ident_t = Tok()
    ones_bf = sb("ones_bf", [128, 128], BF16)
    ones_t = Tok()
    NG = 9
    gains = sb("gains", [128, NG, 8], F32)
    gains_t = Tok()
    epsb = sb("epsb", [128, 1], F32)
    eps_t = Tok()
    nab = sb("nab", [128, 64], F32)
    nab_t = Tok()
    NRP = 20
    rwp = sb("rwp", [128, NRP, 8], F32)
    rwp_t = Tok()
    cst = sb("cst", [128, 8], F32)
    cst_t = Tok()
    masks = sb("masks", [64, 2, 256], F32)
    masks_t = Tok()
    bdones = sb("bdones", [128, 128], BF16)
    onesf = sb("onesf", [128, 64], F32)
    Hh = sb("Hh", [128, 2, 8], F32)
    Hh_t = Tok()
    hnh = sb("hnh", [128, 2, 8], BF16)
    hnh_t = Tok()
    cmask = sb("cmask", [64, 16], F32)
    sel = sb("sel", [64, 4], F32)
    smp_t = Tok()
    arena = sb("arena", [128, ARENA_F32], F32)
    psum = es.enter_context(nc.psum_tensor("psum", [128, 8 * 512], F32))

    class Carver:
        def __init__(self):
            self.off = 0

        def __call__(self, shape, dt):
            n = 1
            for s_ in shape[1:]:
                n *= s_
            nf = (n + 1) // 2 if dt == BF16 else n
            nf = (nf + 7) // 8 * 8
            assert self.off + nf <= ARENA_F32, ("arena overflow", self.off, nf)
            v = arena[:, self.off:self.off + nf]
            self.off += nf
            if dt == BF16:
                v = v.bitcast(BF16)
            v = v[:, 0:n]
            if len(shape) == 3:
                v = v.rearrange("p (a b) -> p a b", a=shape[1])
            elif len(shape) == 4:
                v = v.rearrange("p (a b c) -> p a b c", a=shape[1], b=shape[2])
            return v[0:shape[0]]

        def ring(self, k, shape, dt):
            return Ring([self(shape, dt) for _ in range(k)])

    def bank(i):
        return psum[:, i * 512:(i + 1) * 512]

    def bank_bf(i):
        return psum[:, i * 512:(i + 1) * 512].bitcast(BF16)

    P.op("sp", lambda e: e.dma_start(out=ident[:], in_=ident_d[:, :]), writes=[ident_t], dma=True)
    P.op("dve", lambda e: e.tensor_copy(out=identb[:], in_=ident[:]), reads=[ident_t], writes=[ident_t])
    P.op("pool", lambda e: e.memset(ones_bf[:], 1.0), writes=[ones_t])
    P.op("pool", lambda e: e.memset(epsb[:], EPS), writes=[eps_t])
    gsrc = [ffn_norm_d[0, 0], ffn_norm_d[0, 1], ffn_norm_d[1, 0], ffn_norm_d[1, 1],
            ple_norm_d[0], ple_norm_d[1], final_norm_d.ap(), mix_norm_d[0], mix_norm_d[1]]
    for gi, g in enumerate(gsrc):
        P.op("sp", lambda e, gi=gi, g=g: e.dma_start(out=gains[:, gi, :], in_=g.rearrange("(c p) -> p c", p=128),
                                                     allow_slow_non_contiguous=True),
             writes=[gains_t], dma=True)
    if do_rwkv:
        P.op("pool", lambda e: e.memset(cst[:, 0:1], 1.0), writes=[cst_t])
        P.op("pool", lambda e: e.memset(cst[:, 1:2], -0.5), writes=[cst_t])
        P.op("pool", lambda e: e.memset(cst[:, 2:3], 64e-5), writes=[cst_t])
        P.op("pool", lambda e: e.memset(cst[:, 3:4], 0.0), writes=[cst_t])
        P.op("pool", lambda e: e.memset(onesf[:], 1.0), writes=[cst_t])
        psrc = [rw_mu_d[0, n] for n in range(6)] + [rw_w0_d[0, 0], rw_w0_d[0, 1], rw_a0_d[0, 0], rw_a0_d[0, 1],
                                                    rw_k_k_d[0], rw_k_a_d[0], rw_r_k_d[0].rearrange("h n -> (h n)"),
                                                    rw_lnx_w_d[0], rw_lnx_b_d[0]]
        for gi, g in enumerate(psrc):
            P.op("sp", lambda e, gi=gi, g=g: e.dma_start(out=rwp[:, gi, :], in_=g.rearrange("(c p) -> p c", p=128),
                                                         allow_slow_non_contiguous=True), writes=[rwp_t], dma=True)
        P.op("dve", lambda e: e.tensor_scalar(out=rwp[:, 15:17, :], in0=rwp[:, 6:8, :], scalar1=-1.0, scalar2=None, op0=ALU.mult),
             reads=[rwp_t], writes=[rwp_t])
        P.op("dve", lambda e: e.tensor_scalar(out=rwp[:, 17, :], in0=rwp[:, 11, :], scalar1=-1.0, scalar2=1.0, op0=ALU.mult, op1=ALU.add),
             reads=[rwp_t], writes=[rwp_t])
        P.op("dve", lambda e: e.tensor_scalar(out=rwp[:, 18, :], in0=rwp[:, 17, :], scalar1=2.0, scalar2=None, op0=ALU.mult),
             reads=[rwp_t], writes=[rwp_t])
        for z_ in range(2):
            P.op("sp", lambda e, z_=z_: e.dma_start(out=masks[:, z_, :], in_=masks_d[z_]), writes=[masks_t], dma=True)
        P.op("pool", lambda e: e.dma_start(out=bdones[:], in_=bdones_d[:, :]), writes=[cst_t], dma=True)
    if sample:
        P.op("sp", lambda e: e.dma_start(out=cmask[:], in_=cmask_d[:, :]), writes=[smp_t], dma=True)
        P.op("sp", lambda e: e.dma_start(out=sel[:], in_=sel_d[:, :]), writes=[smp_t], dma=True)
    if do_na:
        P.op("pool", lambda e: e.memset(nab[:], 0.0), writes=[nab_t])
        bq = na_b_qkv_d[0, 0:D]
        bk = na_b_qkv_d[0, D:2 * D]
        bv = na_b_qkv_d[0, 2 * D:3 * D]
        for (src, off) in ((bq, 0), (bk, 16)):
            P.op("sp", lambda e, src=src, off=off: e.dma_start(out=nab[0:96, off:off + 10], in_=src[0:960].rearrange("(t p) -> p t", p=96),
                                                               allow_slow_non_contiguous=True), writes=[nab_t], dma=True)
            P.op("sp", lambda e, src=src, off=off: e.dma_start(out=nab[0:64, off + 10:off + 11], in_=src[960:1024].rearrange("(t p) -> p t", p=64),
                                                               allow_slow_non_contiguous=True), writes=[nab_t], dma=True)
        P.op("sp", lambda e: e.dma_start(out=nab[:, 32:40], in_=bv.rearrange("(c p) -> p c", p=128), allow_slow_non_contiguous=True),
             writes=[nab_t], dma=True)
        P.op("sp", lambda e: e.dma_start(out=nab[:, 40:48], in_=na_b_o_d[0].rearrange("(c p) -> p c", p=128), allow_slow_non_contiguous=True),
             writes=[nab_t], dma=True)
        P.op("dve", lambda e: e.tensor_scalar(out=nab[:, 0:11], in0=nab[:, 0:11], scalar1=float(NA_HD) ** -0.5, scalar2=None, op0=ALU.mult),
             reads=[nab_t], writes=[nab_t])

    win_t = [[[Tok() for _ in range(NJ)] for _ in range(2)] for _ in range(DEPTH)]
    wout_t = [[[Tok() for _ in range(8)] for _ in range(2)] for _ in range(DEPTH)]
    wg_t = [[Tok() for _ in range(8)] for _ in range(DEPTH)]
    wp_t = [[Tok() for _ in range(8)] for _ in range(DEPTH)]
    wqkv_t = [[Tok() for _ in range(NHT)] for _ in range(3)]
    wno_t = [Tok() for _ in range(8)]
    wrkv_t = [[Tok() for _ in range(8)] for _ in range(3)]
    wro_t = [Tok() for _ in range(8)]

    def conv_rwkv():
        for s_ in range(3):
            src = rw_w_rkv_d[0, s_].rearrange("(kc p) (d c) -> d p kc c", p=128, c=128)
            for d in range(8):
                P.op("pool", lambda e, s_=s_, d=d, src=src: e.dma_start(out=wrkv_s[s_][d], in_=src[d]), writes=[wrkv_t[s_][d]], dma=True)
        src2 = rw_w_o_d[0].rearrange("(kc p) (d c) -> d p kc c", p=128, c=128)
        for d in range(8):
            P.op("pool", lambda e, d=d: e.dma_start(out=wro_s[d], in_=src2[d]), writes=[wro_t[d]], dma=True)

    def conv_ffn(i, h):
        src = ffn_w_in_d[i, h].rearrange("(kc p) (g j c) -> g j p kc c", p=128, g=2, j=NJ, c=128)
        for j in range(NJ):
            for g in range(2):
                P.op("pool", lambda e, i=i, h=h, j=j, g=g, src=src: e.dma_start(
                    out=win_s[i][h][j, :, :, g * 128:(g + 1) * 128], in_=src[g, j]),
                    writes=[win_t[i][h][j]], dma=True)
        src2 = ffn_w_out_d[i, h].rearrange("(jc p) (d c) -> d p jc c", p=128, c=128)
        for d in range(8):
            for half in range(2):
                P.op("pool", lambda e, i=i, h=h, d=d, half=half, src2=src2: e.dma_start(
                    out=wout_s[i][h][d, :, half * 11:(half + 1) * 11, :], in_=src2[d, :, half * 11:(half + 1) * 11, :]),
                    writes=[wout_t[i][h][d]], dma=True)

    def conv_ple(i):
        src = ple_w_gate_d[i].rearrange("(kc p) (d c) -> d p kc c", p=128, c=128)
        src2 = ple_w_proj_d[i].rearrange("(kc p) (d c) -> d p kc c", p=128, c=128)
        for d in range(8):
            P.op("pool", lambda e, i=i, d=d, src=src: e.dma_start(out=wg_s[i][d], in_=src[d]),
                 writes=[wg_t[i][d]], dma=True)
            P.op("pool", lambda e, i=i, d=d, src2=src2: e.dma_start(out=wp_s[i][d], in_=src2[d]),
                 writes=[wp_t[i][d]], dma=True)

    def conv_na():
        src = na_w_qkv_d[0].rearrange("(kc p) (s c) -> s p kc c", p=128, s=3)
        for s_ in range(3):
            for ht in range(NHT):
                ncol = 96 if ht < NHT - 1 else 64
                P.op("pool", lambda e, s_=s_, ht=ht, ncol=ncol: e.dma_start(
                    out=wqkv_s[s_][ht, :, :, 0:ncol], in_=src[s_, :, :, ht * 96:ht * 96 + ncol]),
                    writes=[wqkv_t[s_][ht]], dma=True)
        src2 = na_w_o_d[0].rearrange("(kc p) (d c) -> d p kc c", p=128, c=128)
        for d in range(8):
            P.op("pool", lambda e, d=d: e.dma_start(out=wno_s[d], in_=src2[d]), writes=[wno_t[d]], dma=True)

    if do_ffn:
        conv_ffn(0, 0)
        conv_ffn(0, 1)
    if do_rwkv:
        conv_rwkv()
    if do_ple:
        conv_ple(0)
    if do_ffn:
        conv_ffn(1, 0)
    if do_na:
        conv_na()
    if do_ffn:
        conv_ffn(1, 1)
    if do_ple:
        conv_ple(1)

    class NormBufs:
        def __init__(self, cv, stat_bank, nr=2, width=TT):
            self.sq = cv.ring(2, [128, width], BF16)
            self.lnb = cv.ring(1, [128, width], F32)
            self.rstd = cv.ring(nr, [128, width], F32)
            self.st = Ring([bank(stat_bank)])

    def rmsnorm_tile(nb, tt, gi, out_ap_fn, out_tok, src=None):
        if src is None:
            t0 = tt * TT
            src_fn, n, toks_fn = (lambda c: H[:, c, t0:t0 + TT]), TT, (lambda c: [H_tok[c][tt]])
        else:
            src_fn, n, toks_fn = src
        st, st_tok = nb.st.next()
        for c in range(8):
            sqb, sq_tok = nb.sq.next()
            P.op("act", lambda e, c=c, sqb=sqb: e.activation(out=sqb[:, 0:n], in_=src_fn(c), func=AF.Square),
                 reads=toks_fn(c), writes=[sq_tok])
            P.op("pe", lambda e, c=c, sqb=sqb, st=st: e.matmul(st[:, 0:n], ones_bf[:], sqb[:, 0:n], start=(c == 0), stop=(c == 7)),
                 reads=[sq_tok, ones_t], writes=[st_tok])
        lb, lb_tok = nb.lnb.next()
        rs, rs_tok = nb.rstd.next()
        P.op("act", lambda e: e.activation(out=lb[:, 0:n], in_=st[:, 0:n], func=AF.Ln, bias=epsb[:], scale=1.0 / D),
             reads=[st_tok, eps_t], writes=[lb_tok])
        P.op("act", lambda e: e.activation(out=rs[:, 0:n], in_=lb[:, 0:n], func=AF.Exp, scale=-0.5),
             reads=[lb_tok], writes=[rs_tok])
        for c in range(8):
            P.op("dve", lambda e, c=c: e.scalar_tensor_tensor(out=out_ap_fn(c), in0=src_fn(c),
                                                              scalar=gains[:, gi, c:c + 1], in1=rs[:, 0:n],
                                                              op0=ALU.mult, op1=ALU.mult),
                 reads=toks_fn(c) + [gains_t, rs_tok], writes=[out_tok])

    def ffn_phase(i, h, halo=False):
        P.barrier()
        cv = Carver()
        nb = NormBufs(cv, 7)
        xn = cv.ring(2, [128, 8, TT], BF16)
        hid = cv.ring(2, [128, NJ, TT], BF16)
        sg = cv.ring(3, [128, TT], F32)
        wi = cv.ring(3, [128, 8, 256], BF16)
        wo = cv.ring(3, [128, NJ, 128], BF16)
        pbank = Ring([bank(k) for k in range(7)])
        jobs = [(tt, TT) for tt in range(ntt)]
        if halo:
            jobs.append((None, 2))
        for tt, n in jobs:
            if tt is None:
                src = ((lambda c: Hh[:, :, c]), 2, (lambda c: [Hh_t]))
                dst_fn = lambda d: Hh[:, :, d]
                dst_tok = lambda d: Hh_t
            else:
                t0 = tt * TT
                src = None
                dst_fn = lambda d, t0=t0: H[:, d, t0:t0 + TT]
                dst_tok = lambda d, tt=tt: H_tok[d][tt]
            xb, xb_tok = xn.next()
            rmsnorm_tile(nb, tt, 2 * i + h, lambda c, xb=xb, n=n: xb[:, c, 0:n], xb_tok, src)
            hb, hb_tok = hid.next()
            for j in range(NJ):
                wb, wb_tok = wi.next()
                P.op("sp", lambda e, j=j, wb=wb: e.dma_start(out=wb, in_=win_s[i][h][j]),
                     reads=[win_t[i][h][j]], writes=[wb_tok], dma=True)
                pg, pg_tok = pbank.next()
                pu, pu_tok = pbank.next()
                for kc in range(8):
                    P.op("pe", lambda e, kc=kc, wb=wb, pg=pg, xb=xb, n=n: e.matmul(pg[:, 0:n], wb[:, kc, 0:128], xb[:, kc, 0:n],
                                                                                 start=(kc == 0), stop=(kc == 7)),
                         reads=[wb_tok, xb_tok], writes=[pg_tok])
                for kc in range(8):
                    P.op("pe", lambda e, kc=kc, wb=wb, pu=pu, xb=xb, n=n: e.matmul(pu[:, 0:n], wb[:, kc, 128:256], xb[:, kc, 0:n],
                                                                                 start=(kc == 0), stop=(kc == 7)),
                         reads=[wb_tok, xb_tok], writes=[pu_tok])
                sgb, sg_tok = sg.next()
                P.op("act", lambda e, pg=pg, sgb=sgb, n=n: e.activation(out=sgb[:, 0:n], in_=pg[:, 0:n], func=AF.Silu),
                     reads=[pg_tok], writes=[sg_tok])
                P.op("dve", lambda e, j=j, pu=pu, sgb=sgb, hb=hb, n=n: e.tensor_tensor(out=hb[:, j, 0:n], in0=sgb[:, 0:n], in1=pu[:, 0:n], op=ALU.mult),
                     reads=[sg_tok, pu_tok], writes=[hb_tok])
            for d in range(8):
                wb, wb_tok = wo.next()
                P.op("sp", lambda e, d=d, wb=wb: e.dma_start(out=wb, in_=wout_s[i][h][d]),
                     reads=[wout_t[i][h][d]], writes=[wb_tok], dma=True)
                po, po_tok = pbank.next()
                for j in range(NJ):
                    P.op("pe", lambda e, j=j, wb=wb, po=po, hb=hb, n=n: e.matmul(po[:, 0:n], wb[:, j, :], hb[:, j, 0:n],
                                                                               start=(j == 0), stop=(j == NJ - 1)),
                         reads=[wb_tok, hb_tok], writes=[po_tok])
                P.op("dve", lambda e, d=d, po=po, n=n, dst_fn=dst_fn: e.scalar_tensor_tensor(out=dst_fn(d), in0=po[:, 0:n], scalar=0.5,
                                                                                          in1=dst_fn(d), op0=ALU.mult, op1=ALU.add),
                     reads=[po_tok, dst_tok(d)], writes=[dst_tok(d)])

    def ple_phase(i, seg):
        P.barrier()
        cv = Carver()
        nb = NormBufs(cv, 7)
        xn = cv.ring(2, [128, 8, TT], BF16)
        sg = cv.ring(3, [128, TT], F32)
        wgb = cv.ring(2, [128, 8, 128], BF16)
        wpb = cv.ring(2, [128, 2, 128], BF16)
        pst = cv.ring(2, [128, 4, PLE], F32)
        pT = cv.ring(2, [128, 2, TT], BF16)
        pbank = Ring([bank(k) for k in range(7)])
        for tt in range(ntt):
            t0 = tt * TT
            g0 = seg * T + t0
            xb, xb_tok = xn.next()
            rmsnorm_tile(nb, tt, 4 + i, lambda c, xb=xb: xb[:, c, :], xb_tok)
            pb, pb_tok = pst.next()
            P.op("sp", lambda e, pb=pb, g0=g0: e.dma_start(out=pb, in_=p_d[i, g0:g0 + TT, :].rearrange("(k q) f -> q k f", q=128)),
                 writes=[pb_tok], dma=True)
            ptb, pt_tok = pT.next()
            for kc in range(2):
                pp, pp_tok = pbank.next()
                for k in range(4):
                    P.op("pe", lambda e, kc=kc, k=k, pp=pp, pb=pb: e.transpose(pp[:, k * 128:(k + 1) * 128],
                                                                             pb[:, k, kc * 128:(kc + 1) * 128], ident[:]),
                         reads=[pb_tok, ident_t], writes=[pp_tok])
                P.op("act", lambda e, kc=kc, pp=pp, ptb=ptb: e.copy(out=ptb[:, kc, :], in_=pp),
                     reads=[pp_tok], writes=[pt_tok])
            for d in range(8):
                wgt, wgt_tok = wgb.next()
                wpt, wpt_tok = wpb.next()
                P.op("sp", lambda e, d=d, wgt=wgt: e.dma_start(out=wgt, in_=wg_s[i][d]),
                     reads=[wg_t[i][d]], writes=[wgt_tok], dma=True)
                P.op("sp", lambda e, d=d, wpt=wpt: e.dma_start(out=wpt, in_=wp_s[i][d]),
                     reads=[wp_t[i][d]], writes=[wpt_tok], dma=True)
                pg, pg_tok = pbank.next()
                pq, pq_tok = pbank.next()
                for kc in range(8):
                    P.op("pe", lambda e, kc=kc, wgt=wgt, pg=pg, xb=xb: e.matmul(pg, wgt[:, kc, :], xb[:, kc, :],
                                                                              start=(kc == 0), stop=(kc == 7)),
                         reads=[wgt_tok, xb_tok], writes=[pg_tok])
                for kc in range(2):
                    P.op("pe", lambda e, kc=kc, wpt=wpt, pq=pq, ptb=ptb: e.matmul(pq, wpt[:, kc, :], ptb[:, kc, :],
                                                                                start=(kc == 0), stop=(kc == 1)),
                         reads=[wpt_tok, pt_tok], writes=[pq_tok])
                sgb, sg_tok = sg.next()
                P.op("act", lambda e, pg=pg, sgb=sgb: e.activation(out=sgb, in_=pg, func=AF.Sigmoid),
                     reads=[pg_tok], writes=[sg_tok])
                P.op("dve", lambda e, pq=pq, sgb=sgb: e.tensor_tensor(out=sgb, in0=sgb, in1=pq, op=ALU.mult),
                     reads=[sg_tok, pq_tok], writes=[sg_tok])
                P.op("dve", lambda e, d=d, sgb=sgb, t0=t0: e.tensor_tensor(out=H[:, d, t0:t0 + TT], in0=H[:, d, t0:t0 + TT],
                                                                        in1=sgb, op=ALU.add),
                     reads=[sg_tok, H_tok[d][tt]], writes=[H_tok[d][tt]])

    def na_phase(seg):
        P.barrier()
        cv = Carver()
        is_smp = sample and seg == nseg - 1
        RO = 4 if is_smp else 0
        RS = R + 2 * RO
        nb = NormBufs(cv, 7)
        hn = cv([128, 8, T], BF16)
        hn_tok = [Tok() for _ in range(ntt)]
        Otok = cv([128, R // 2, D], BF16)
        Otok_t = [Tok() for _ in range(R // 2)]
        Qt = cv([96, T], BF16)
        Kt = cv([96, RS * 64], BF16)
        Qt_t, Kt_t = Tok(), Tok()
        V2 = [cv([128, RS // 2, 96], BF16) for _ in range(2)]
        V2_t = [Tok(), Tok()]
        Bt = cv.ring(2, [64, 960], BF16)
        pr = cv.ring(2, [64, 512], BF16)
        prT = cv.ring(2, [128, 4, 64], BF16)
        small = cv.ring(8, [64, 4], F32)
        wq = cv.ring(2, [128, 8, 96], BF16)
        wk = cv.ring(2, [128, 8, 96], BF16)
        wv = cv.ring(2, [128, 8, 96], BF16)
        Ofm = cv.ring(1, [128, 8, TT], BF16)
        wo = cv.ring(2, [128, 8, 128], BF16)
        scb = Ring([bank(0), bank(1)])
        ptb = Ring([bank_bf(2), bank_bf(3)])
        pob = Ring([psum[:, 4 * 512 + k * 64:4 * 512 + (k + 1) * 64] for k in range(8)])
        pjb = Ring([bank(5), bank(6)])
        for tt in range(ntt):
            rmsnorm_tile(nb, tt, 8, lambda c, tt=tt: hn[:, c, tt * TT:(tt + 1) * TT], hn_tok[tt])

        def load_w(ht, which):
            res = {}
            for s_, ring in ((0, wq), (1, wk), (2, wv)):
                if s_ not in which:
                    continue
                wt, w_tok = ring.next()
                P.op("sp", lambda e, wt=wt, s_=s_, ht=ht: e.dma_start(out=wt, in_=wqkv_s[s_][ht]), reads=[wqkv_t[s_][ht]], writes=[w_tok], dma=True)
                res[s_] = (wt, w_tok)
            return res

        def proj_fm(ht, ncol, wt, w_tok, dst, dst_t, boff, scale):
            for tt in range(ntt):
                pj, pj_tok = pjb.next()
                for kc in range(8):
                    P.op("pe", lambda e, kc=kc, wt=wt, pj=pj, tt=tt, ncol=ncol: e.matmul(
                        pj[0:ncol, :], wt[:, kc, 0:ncol], hn[:, kc, tt * TT:(tt + 1) * TT], start=(kc == 0), stop=(kc == 7)),
                        reads=[w_tok, hn_tok[tt]], writes=[pj_tok])
                P.op("act", lambda e, pj=pj, dst=dst, tt=tt, ncol=ncol, boff=boff, scale=scale, ht=ht: e.activation(
                    out=dst[0:ncol, tt * TT:(tt + 1) * TT], in_=pj[0:ncol, :], func=AF.Identity,
                    bias=nab[0:ncol, boff + ht:boff + ht + 1], scale=scale),
                    reads=[pj_tok, nab_t], writes=[dst_t])

        def proj_v(ht, ncol, wvt, wv_tok, pars):
            for par in pars:
                nm = R // 2 - par
                for m0 in range(0, nm, 4):
                    mm = min(4, nm - m0)
                    pj, pj_tok = pjb.next()
                    for m in range(mm):
                        st_ = par * 64 + 128 * (m0 + m)
                        for kc in range(8):
                            P.op("pe", lambda e, kc=kc, wvt=wvt, pj=pj, m=m, st_=st_, ncol=ncol: e.matmul(
                                pj[:, m * 96:m * 96 + ncol], hn[:, kc, st_:st_ + 128], wvt[:, kc, 0:ncol],
                                start=(kc == 0), stop=(kc == 7)),
                                reads=[wv_tok] + hn_tok, writes=[pj_tok])
                    P.op("dve", lambda e, pj=pj, par=par, m0=m0, mm=mm, ncol=ncol: e.tensor_copy(
                        out=V2[par][:, m0:m0 + mm, 0:ncol],
                        in_=pj[:, 0:mm * 96].rearrange("p (m c) -> p m c", c=96)[:, :, 0:ncol]),
                        reads=[pj_tok], writes=[V2_t[par]])

        def attend(ht, nh):
            for hl in range(nh):
                h = 3 * ht + hl
                btile, bt_tok = Bt.next()
                P.op("pool", lambda e, btile=btile, h=h: e.dma_start(out=btile, in_=btab_d[h]), writes=[bt_tok], dma=True)
                for i in range(R):
                    if not is_smp:
                        rs0 = min(max(i - KH // 2, 0), R - KH)
                        variants = [(rs0, rs0 - i + 7, None)]
                    elif i < 4:
                        variants = [(RO, 7 - i, 0), (i, 3, 1)]
                    elif i >= R - 4:
                        variants = [(RO + R - 8, (R - 8) - i + 7, 2), (i, 3, 3)]
                    else:
                        variants = [(i, 3, None)]
                    po, po_tok = pob.next()
                    sm, sm_tok = small.next()
                    for vi, (rs, ri0, selc) in enumerate(variants):
                        sc, sc_tok = scb.next()
                        P.op("pe", lambda e, sc=sc, hl=hl, i=i, rs=rs: e.matmul(
                            sc[0:64, :], Qt[32 * hl:32 * hl + 32, i * 64:(i + 1) * 64], Kt[32 * hl:32 * hl + 32, rs * 64:(rs + 8) * 64],
                            start=True, stop=False), reads=[Qt_t, Kt_t], writes=[sc_tok])
                        P.op("pe", lambda e, sc=sc, btile=btile, ri0=ri0: e.matmul(
                            sc[0:64, :], identb[0:64, 0:64], btile[:, ri0 * 64:(ri0 + 8) * 64], start=False, stop=True),
                            reads=[bt_tok, ident_t], writes=[sc_tok])
                        if vi > 0:
                            sm, sm_tok = small.next()
                        P.op("dve", lambda e, sc=sc, sm=sm: e.tensor_reduce(out=sm[:, 0:1], in_=sc[0:64, :], axis=AX, op=ALU.max, negate=True),
                             reads=[sc_tok], writes=[sm_tok])
                        prb, pr_tok = pr.next()
                        P.op("act", lambda e, sc=sc, sm=sm, prb=prb: e.activation(out=prb, in_=sc[0:64, :], func=AF.Exp, bias=sm[:, 0:1],
                                                                               scale=1.0, accum_out=sm[:, 1:2]),
                             reads=[sc_tok, sm_tok], writes=[pr_tok, sm_tok])
                        P.op("dve", lambda e, sm=sm: e.reciprocal(out=sm[:, 2:3], in_=sm[:, 1:2]), reads=[sm_tok], writes=[sm_tok])
                        if selc is not None:
                            P.op("dve", lambda e, sm=sm, selc=selc: e.tensor_tensor(out=sm[:, 2:3], in0=sm[:, 2:3], in1=sel[:, selc:selc + 1], op=ALU.mult),
                                 reads=[sm_tok, smp_t], writes=[sm_tok])
                            P.op("dve", lambda e, sm=sm, prb=prb: e.tensor_scalar(out=prb, in0=prb, scalar1=sm[:, 2:3], scalar2=None, op0=ALU.mult),
                                 reads=[sm_tok, pr_tok], writes=[pr_tok])
                        pt, pt_tok = ptb.next()
                        for k in range(4):
                            P.op("pe", lambda e, k=k, pt=pt, prb=prb: e.transpose(pt[:, k * 64:(k + 1) * 64], prb[:, k * 128:(k + 1) * 128],
                                                                                identb[0:64, 0:64]),
                                 reads=[pr_tok, ident_t], writes=[pt_tok])
                        prt, prt_tok = prT.next()
                        if i % 2:
                            P.op("act", lambda e, pt=pt, prt=prt: e.copy(out=prt, in_=pt[:, 0:256].rearrange("p (k q) -> p k q", q=64)),
                                 reads=[pt_tok], writes=[prt_tok])
                        else:
                            P.op("dve", lambda e, pt=pt, prt=prt: e.tensor_copy(out=prt, in_=pt[:, 0:256].rearrange("p (k q) -> p k q", q=64)),
                                 reads=[pt_tok], writes=[prt_tok])
                        for kt in range(4):
                            P.op("pe", lambda e, kt=kt, po=po, prt=prt, rs=rs, hl=hl, vi=vi, nv=len(variants): e.matmul(
                                po[0:64, 0:32], prt[:, kt, :], V2[rs % 2][:, rs // 2 + kt, 32 * hl:32 * hl + 32],
                                start=(kt == 0 and vi == 0), stop=(kt == 3 and vi == nv - 1)),
                                reads=[prt_tok, V2_t[rs % 2]], writes=[po_tok])
                    pb_ = (i % 2) * 64
                    if variants[0][2] is None:
                        P.op("dve", lambda e, po=po, sm=sm, pb_=pb_, i=i, h=h: e.tensor_scalar(
                            out=Otok[pb_:pb_ + 64, i // 2, 32 * h:32 * h + 32], in0=po[0:64, 0:32], scalar1=sm[:, 2:3], scalar2=None, op0=ALU.mult),
                            reads=[po_tok, sm_tok], writes=[Otok_t[i // 2]])
                    else:
                        P.op("dve", lambda e, po=po, pb_=pb_, i=i, h=h: e.tensor_copy(
                            out=Otok[pb_:pb_ + 64, i // 2, 32 * h:32 * h + 32], in_=po[0:64, 0:32]),
                            reads=[po_tok], writes=[Otok_t[i // 2]])

        if not is_smp:
            for ht in range(NHT):
                nh = 3 if ht < NHT - 1 else 2
                ncol = 32 * nh
                ws = load_w(ht, (0, 1, 2))
                proj_fm(ht, ncol, ws[0][0], ws[0][1], Qt, Qt_t, 0, float(NA_HD) ** -0.5)
                proj_fm(ht, ncol, ws[1][0], ws[1][1], Kt, Kt_t, 16, 1.0)
                proj_v(ht, ncol, ws[2][0], ws[2][1], (0, 1))
                attend(ht, nh)
        else:
            KV_t = [Tok() for _ in range(NHT)]
            hal_t = Tok()
            for side, c0 in ((0, 0), (1, 256 + T)):
                P.op("pool", lambda e, side=side, c0=c0: e.dma_start(out=K_all[:, c0:c0 + 256], in_=kvh_k_d[side]), writes=[hal_t], dma=True)
                P.op("pool", lambda e, side=side, c0=c0: e.dma_start(out=V_all[c0:c0 + 256, :], in_=kvh_v_d[side]), writes=[hal_t], dma=True)
            for ht in range(NHT):
                nh = 3 if ht < NHT - 1 else 2
                ncol = 32 * nh
                ws = load_w(ht, (1, 2))
                proj_fm(ht, ncol, ws[1][0], ws[1][1], Kt, Kt_t, 16, 1.0)
                proj_v(ht, ncol, ws[2][0], ws[2][1], (0,))
                P.op("sp", lambda e, ht=ht, ncol=ncol: e.dma_start(out=K_all[ht * 96:ht * 96 + ncol, 256:256 + T], in_=Kt[0:ncol, 0:T]),
                     reads=[Kt_t], writes=[KV_t[ht]], dma=True)
                P.op("sp", lambda e, ht=ht, ncol=ncol: e.dma_start(
                    out=V_all[256:256 + T, ht * 96:ht * 96 + ncol].rearrange("(m p) c -> p m c", p=128), in_=V2[0][:, 0:R // 2, 0:ncol]),
                    reads=[V2_t[0]], writes=[KV_t[ht]], dma=True)
            for side, c0 in ((0, 256), (1, T)):
                P.op("pool", lambda e, side=side, c0=c0: e.dma_start(out=kvo_k[side], in_=K_all[:, c0:c0 + 256]), reads=KV_t, dma=True)
                P.op("pool", lambda e, side=side, c0=c0: e.dma_start(out=kvo_v[side], in_=V_all[c0:c0 + 256, :]), reads=KV_t, dma=True)
            for ht in range(NHT):
                nh = 3 if ht < NHT - 1 else 2
                ncol = 32 * nh
                ws = load_w(ht, (0,))
                proj_fm(ht, ncol, ws[0][0], ws[0][1], Qt, Qt_t, 0, float(NA_HD) ** -0.5)
                P.op("sp", lambda e, ht=ht, ncol=ncol: e.dma_start(out=Kt[0:ncol, :], in_=K_all[ht * 96:ht * 96 + ncol, :]),
                     reads=[KV_t[ht], hal_t], writes=[Kt_t], dma=True)
                for par in range(2):
                    nm = RS // 2 - par
                    P.op("sp", lambda e, ht=ht, ncol=ncol, par=par, nm=nm: e.dma_start(
                        out=V2[par][:, 0:nm, 0:ncol],
                        in_=V_all[par * 64:par * 64 + nm * 128, ht * 96:ht * 96 + ncol].rearrange("(m p) c -> p m c", p=128)),
                        reads=[KV_t[ht], hal_t], writes=[V2_t[par]], dma=True)
                attend(ht, nh)
        for tt in range(ntt):
            t0 = tt * TT
            of, of_tok = Ofm.next()
            for m in range(4):
                rp = tt * 4 + m
                pt, pt_tok = ptb.next()
                for c in range(8):
                    P.op("pe", lambda e, c=c, pt=pt, rp=rp: e.transpose(pt[:, c * 128:(c + 1) * 128], Otok[:, rp, c * 128:(c + 1) * 128], identb[:]),
                         reads=[Otok_t[rp], ident_t], writes=[pt_tok])
                P.op("dve", lambda e, pt=pt, of=of, m=m: e.tensor_tensor(
                    out=of[:, :, m * 128:(m + 1) * 128], in0=pt.rearrange("p (c q) -> p c q", q=128),
                    in1=nab[:, 32:40].unsqueeze(2).broadcast_to([128, 8, 128]), op=ALU.add),
                    reads=[pt_tok, nab_t], writes=[of_tok])
            for d in range(8):
                wb, wb_tok = wo.next()
                P.op("sp", lambda e, d=d, wb=wb: e.dma_start(out=wb, in_=wno_s[d]), reads=[wno_t[d]], writes=[wb_tok], dma=True)
                pj, pj_tok = pjb.next()
                for kc in range(8):
                    P.op("pe", lambda e, kc=kc, wb=wb, pj=pj, of=of: e.matmul(pj, wb[:, kc, :], of[:, kc, :], start=(kc == 0), stop=(kc == 7)),
                         reads=[wb_tok, of_tok], writes=[pj_tok])
                P.op("dve", lambda e, d=d, pj=pj, t0=t0: e.scalar_tensor_tensor(out=H[:, d, t0:t0 + TT], in0=pj, scalar=nab[:, 40 + d:41 + d],
                                                                             in1=H[:, d, t0:t0 + TT], op0=ALU.add, op1=ALU.add),
                     reads=[pj_tok, nab_t, H_tok[d][tt]], writes=[H_tok[d][tt]])

    def rwkv_phase(seg):
        P.barrier()
        cv = Carver()
        TW = 256
        ntw = T // TW
        nb = NormBufs(cv, 7, 1, 264)
        Yb = cv([128, 8, T], BF16)
        Yb_t = [Tok() for _ in range(ntw)]
        w1c = cv([128, 8, 64], BF16)
        a1s = [cv([128, 8, 64], BF16) for _ in range(2)]
        g1s = cv([128, 8, 128], BF16)
        w2c = cv([64, D], BF16)
        lwc_t = Tok()
        a2s = [cv([64, D], BF16) for _ in range(2)]
        g2s = cv([128, D], BF16)
        lw_t = Tok()
        for z_ in range(2):
            P.op("pool", lambda e, z_=z_: e.dma_start(out=a1s[z_], in_=rw_a1_d[0, z_].rearrange("(kc p) r -> p kc r", p=128)), writes=[lw_t], dma=True)
            P.op("pool", lambda e, z_=z_: e.dma_start(out=a2s[z_], in_=rw_a2_d[0, z_]), writes=[lw_t], dma=True)
        P.op("pool", lambda e: e.dma_start(out=g1s, in_=rw_g1_d[0].rearrange("(kc p) r -> p kc r", p=128)), writes=[lw_t], dma=True)
        P.op("pool", lambda e: e.dma_start(out=g2s, in_=rw_g2_d[0]), writes=[lw_t], dma=True)
        S32 = cv([64, 16, 64], F32)
        Sb = cv([64, 16, 64], BF16)
        S_t = [Tok() for _ in range(8)]
        is_smp = sample and seg == nseg - 1
        if is_smp:
            F32s = cv([64, 16, 64], F32)
            Fb = cv([64, 16, 64], BF16)
        rmsnorm_tile(nb, None, 7, lambda c: hnh[:, :, c], hnh_t, ((lambda c: Hh[:, :, c]), 2, (lambda c: [Hh_t])))
        hx = cv.ring(2, [128, 8, TW + 2], BF16)
        dlt = cv.ring(1, [128, 8, TW], BF16)
        xmp = [cv.ring(1, [128, 8, TW], BF16) for _ in range(3)]
        xmt = cv.ring(1, [128, 8, TW], BF16)
        twb = cv.ring(1, [64, TW], BF16)
        tab = [cv.ring(1, [64, TW], BF16) for _ in range(2)]
        tgb = cv.ring(1, [128, TW], BF16)
        wrk = [cv.ring(1, [128, 8, 128], BF16) for _ in range(3)]
        wk_off = cv.off
        R32 = cv([128, TW], F32); K32 = cv([128, TW], F32)
        T1 = cv([128, TW], F32); EW = cv([128, TW], F32); AA = cv([128, TW], F32); AA1 = cv([128, TW], F32)
        T2 = cv([128, TW], F32); KD = cv([128, TW], F32); KK = cv([128, TW], F32); T3 = cv([128, TW], F32)
        BE = cv([128, TW], F32); LL = cv([128, TW], F32); PX = cv([128, TW], F32); PD = T1
        EE = cv.ring(1, [128, TW], F32)
        SQ = cv([128, TW], BF16)
        BV = cv([128, TW], F32); G32 = cv([128, TW], F32); YS = cv([128, TW], F32); YC = T2
        work_t = Tok()
        fin_t = Tok()
        ARr = cv.ring(1, [128, 4, 2, 64], BF16)
        KTr = cv.ring(1, [128, TW], BF16)
        BTr = cv.ring(1, [128, TW], BF16)
        KHr = cv.ring(1, [128, TW], BF16)
        NBHr = cv.ring(1, [128, TW], BF16)
        V16r = cv.ring(1, [128, TW], BF16)
        XTr = cv.ring(1, [64, 3, 4, 128], BF16)
        WCRr = cv.ring(2, [64, 8], F32)
        ITr = cv.ring(1, [64, 8, 512], BF16)
        MBr = cv.ring(1, [64, 8, 64], BF16)
        PRr = cv.ring(1, [64, 8, 128], BF16)
        O16 = cv.ring(1, [128, 8, TW], BF16)
        wro = cv.ring(1, [128, 8, 128], BF16)
        half = Ring([psum[:, (5 + k) * 512:(5 + k) * 512 + 256] for k in range(2)])
        b03 = [bank(k) for k in range(4)]
        b03_t = [Tok() for _ in range(4)]
        b4 = bank(4)
        b4_t = Tok()

        def hx_tile(tw):
            t0 = tw * TW
            hb, hb_tok = hx.next()
            lo = max(t0 - 1, 0)
            hi = min(t0 + TW + 1, T)
            n = hi - lo
            j0 = lo - (t0 - 1)
            toks = sorted(set([lo // TT, (hi - 1) // TT]))
            rd = [H_tok[c][tt] for c in range(8) for tt in toks]
            st, st_tok = nb.st.next()
            for c in range(8):
                sqb, sq_tok = nb.sq.next()
                P.op("act", lambda e, c=c, sqb=sqb: e.activation(out=sqb[:, 0:n], in_=H[:, c, lo:hi], func=AF.Square),
                     reads=rd, writes=[sq_tok])
                P.op("pe", lambda e, c=c, sqb=sqb, st=st: e.matmul(st[:, 0:n], ones_bf[:], sqb[:, 0:n], start=(c == 0), stop=(c == 7)),
                     reads=[sq_tok, ones_t], writes=[st_tok])
            lb, lb_tok = nb.lnb.next()
            rs, rs_tok = nb.rstd.next()
            P.op("act", lambda e: e.activation(out=lb[:, 0:n], in_=st[:, 0:n], func=AF.Ln, bias=epsb[:], scale=1.0 / D),
                 reads=[st_tok, eps_t], writes=[lb_tok])
            P.op("act", lambda e: e.activation(out=rs[:, 0:n], in_=lb[:, 0:n], func=AF.Exp, scale=-0.5),
                 reads=[lb_tok], writes=[rs_tok])
            if j0 > 0:
                P.op("pool", lambda e: e.tensor_copy(out=hb[:, :, 0:1], in_=hnh[:, 0, :].unsqueeze(2)), reads=[hnh_t], writes=[hb_tok])
            if j0 + n < TW + 2:
                P.op("pool", lambda e: e.tensor_copy(out=hb[:, :, TW + 1:TW + 2], in_=hnh[:, 1, :].unsqueeze(2)), reads=[hnh_t], writes=[hb_tok])
            for c in range(8):
                P.op("dve", lambda e, c=c: e.scalar_tensor_tensor(out=hb[:, c, j0:j0 + n], in0=H[:, c, lo:hi],
                                                                  scalar=gains[:, 7, c:c + 1], in1=rs[:, 0:n],
                                                                  op0=ALU.mult, op1=ALU.mult),
                     reads=rd + [gains_t, rs_tok], writes=[hb_tok])
            return hb, hb_tok

        def mix(n_, hb, hb_tok, db, db_tok, ring):
            xb, xb_tok = ring.next()
            P.op("dve", lambda e: e.tensor_tensor(out=xb, in0=db, in1=rwp[:, n_, :].unsqueeze(2).broadcast_to([128, 8, TW]), op=ALU.mult),
                 reads=[db_tok, rwp_t], writes=[xb_tok])
            P.op("dve", lambda e: e.tensor_tensor(out=xb, in0=xb, in1=hb[:, :, 1:TW + 1], op=ALU.add),
                 reads=[hb_tok, xb_tok], writes=[xb_tok])
            return xb, xb_tok

        if rw_stop <= -1:
            return
        dbg_z, dbg_tw = 1, ntw - 1
        for z in (1, 0):
            final = (z == 0)
            if dbg and seg == 0 and z == 0:
                P.op("pool", lambda e: e.dma_start(out=dbg_yb[:, :, :], in_=Yb), reads=Yb_t, dma=True)
            mz = masks[:, z, :]
            P.op("pool", lambda e, z=z: e.dma_start(out=w1c, in_=rw_w1_d[0, z].rearrange("(kc p) r -> p kc r", p=128)), writes=[lwc_t], dma=True)
            P.op("pool", lambda e, z=z: e.dma_start(out=w2c, in_=rw_w2_d[0, z]), writes=[lwc_t], dma=True)
            P.op("pool", lambda e: e.memset(S32, 0.0), writes=S_t)
            if is_smp:
                S2 = S32.rearrange("p h v -> p (h v)")
                QJ = arena[0:64, wk_off:wk_off + 1024]
                FJ = arena[0:64, wk_off + 1024:wk_off + 2048]
                TM = arena[0:64, wk_off + 2048:wk_off + 3072]
                for jc in (range(8) if z == 0 else range(7, -1, -1)):
                    P.op("sp", lambda e, jc=jc, z=z: e.dma_start(out=QJ, in_=summ_all_d[jc, z, 0]), reads=[work_t], writes=[work_t], dma=True)
                    P.op("sp", lambda e, jc=jc, z=z: e.dma_start(out=FJ, in_=summ_all_d[jc, z, 1]), reads=[work_t], writes=[work_t], dma=True)
                    for hd in range(16):
                        bk = hd // 8
                        P.op("pe", lambda e, hd=hd, bk=bk: e.matmul(b03[bk][0:64, (hd % 8) * 64:(hd % 8 + 1) * 64], FJ[:, hd * 64:(hd + 1) * 64], S32[:, hd, :],
                                                               start=True, stop=True), reads=[work_t] + S_t, writes=[b03_t[bk]])
                    for bk in range(2):
                        P.op("dve", lambda e, bk=bk: e.tensor_tensor(out=TM[:, bk * 512:(bk + 1) * 512], in0=b03[bk][0:64, :], in1=QJ[:, bk * 512:(bk + 1) * 512], op=ALU.add),
                             reads=[b03_t[bk], work_t], writes=[work_t])
                    P.op("dve", lambda e: e.tensor_tensor(out=TM, in0=TM, in1=S2, op=ALU.subtract), reads=[work_t] + S_t, writes=[work_t])
                    P.op("dve", lambda e, jc=jc, z=z: e.scalar_tensor_tensor(out=S2, in0=TM, scalar=cmask[:, z * 8 + jc:z * 8 + jc + 1], in1=S2, op0=ALU.mult, op1=ALU.add),
                         reads=[work_t, smp_t] + S_t, writes=S_t)
                P.op("dve", lambda e: e.tensor_copy(out=Sb, in_=S32), reads=S_t, writes=S_t)
                P.op("dve", lambda e: e.tensor_copy(out=F32s, in_=ident[0:64, 0:64].unsqueeze(1).broadcast_to([64, 16, 64])), reads=[ident_t] + S_t, writes=S_t)
                P.op("dve", lambda e: e.tensor_copy(out=Fb, in_=F32s), reads=S_t, writes=S_t)
            else:
                P.op("pool", lambda e: e.memset(Sb, 0.0), writes=S_t)
            tiles = list(range(ntw)) if z == 0 else list(range(ntw - 1, -1, -1))
            pre_hx = None
            for ti, tw in enumerate(tiles + [None]):
                if tw is None:
                    if is_smp:
                        P.op("sp", lambda e, z=z: e.dma_start(out=summ_o[z, 0], in_=S32.rearrange("p h v -> p (h v)")), reads=S_t, dma=True)
                        P.op("sp", lambda e, z=z: e.dma_start(out=summ_o[z, 1], in_=F32s.rearrange("p h v -> p (h v)")), reads=S_t, dma=True)
                    break
                t0 = tw * TW
                tt = t0 // TT
                if pre_hx is None:
                    hb, hb_tok = hx_tile(tw)
                else:
                    hb, hb_tok = pre_hx
                    pre_hx = None
                db, db_tok = dlt.next()
                P.op("dve", lambda e, hb=hb, db=db: e.tensor_tensor(out=db, in0=hb[:, :, 0:TW], in1=hb[:, :, 2:TW + 2], op=ALU.add),
                     reads=[hb_tok], writes=[db_tok])
                P.op("dve", lambda e, hb=hb, db=db: e.scalar_tensor_tensor(out=db, in0=db, scalar=0.5, in1=hb[:, :, 1:TW + 1],
                                                                         op0=ALU.mult, op1=ALU.subtract),
                     reads=[hb_tok, db_tok], writes=[db_tok])
                xr, xr_tok = mix(0, hb, hb_tok, db, db_tok, xmp[0])
                xk, xk_tok = mix(1, hb, hb_tok, db, db_tok, xmp[1])
                xv, xv_tok = mix(2, hb, hb_tok, db, db_tok, xmp[2])
                if rw_stop <= 0:
                    continue
                xw, xw_tok = mix(3, hb, hb_tok, db, db_tok, xmt)
                hp, hp_tok = half.next()
                for kc in range(8):
                    P.op("pe", lambda e, kc=kc, hp=hp, xw=xw, z=z: e.matmul(hp[0:64, :], w1c[:, kc, :], xw[:, kc, :], start=(kc == 0), stop=(kc == 7)),
                         reads=[lwc_t, xw_tok], writes=[hp_tok])
                if rw_stop <= 0.3:
                    continue
                twt, tw_tok = twb.next()
                P.op("act", lambda e, hp=hp, twt=twt: e.activation(out=twt, in_=hp[0:64, :], func=AF.Tanh), reads=[hp_tok], writes=[tw_tok])
                if rw_stop <= 0.5:
                    continue
                xa, xa_tok = mix(4, hb, hb_tok, db, db_tok, xmt)
                tat = {}
                for z2 in ((0, 1) if final else (z,)):
                    hp, hp_tok = half.next()
                    for kc in range(8):
                        P.op("pe", lambda e, kc=kc, hp=hp, xa=xa, z2=z2: e.matmul(hp[0:64, :], a1s[z2][:, kc, :], xa[:, kc, :], start=(kc == 0), stop=(kc == 7)),
                             reads=[lw_t, xa_tok], writes=[hp_tok])
                    if rw_stop <= 0.6:
                        continue
                    tb, tb_tok = tab[z2].next()
                    P.op("dve", lambda e, hp=hp, tb=tb: e.tensor_copy(out=tb, in_=hp[0:64, :]), reads=[hp_tok], writes=[tb_tok])
                    tat[z2] = (tb, tb_tok)
                if rw_stop <= 0.7:
                    continue
                if final:
                    xg, xg_tok = mix(5, hb, hb_tok, db, db_tok, xmt)
                    hp, hp_tok = half.next()
                    for kc in range(8):
                        P.op("pe", lambda e, kc=kc, hp=hp, xg=xg: e.matmul(hp, g1s[:, kc, :], xg[:, kc, :], start=(kc == 0), stop=(kc == 7)),
                             reads=[lw_t, xg_tok], writes=[hp_tok])
                    tgt, tg_tok = tgb.next()
                    P.op("act", lambda e, hp=hp, tgt=tgt: e.activation(out=tgt, in_=hp, func=AF.Sigmoid), reads=[hp_tok], writes=[tg_tok])
                if rw_stop <= 1:
                    continue
                if final:
                    ob, ob_tok = O16.next()
                for fc in range(8):
                    fs = slice(fc * 128, (fc + 1) * 128)
                    wts = []
                    for s_ in range(3):
                        wt, wt_tok = wrk[s_].next()
                        P.op("sp", lambda e, wt=wt, s_=s_, fc=fc: e.dma_start(out=wt, in_=wrkv_s[s_][fc]), reads=[wrkv_t[s_][fc]], writes=[wt_tok], dma=True)
                        wts.append((wt, wt_tok))
                    R32_t, K32_t = Tok(), Tok()
                    V16, v16_tok = V16r.next()
                    for s_, (xm_, xm_tok) in enumerate(((xr, xr_tok), (xk, xk_tok), (xv, xv_tok))):
                        hp, hp_tok = half.next()
                        wt, wt_tok = wts[s_]
                        for kc in range(8):
                            P.op("pe", lambda e, kc=kc, hp=hp, wt=wt, xm_=xm_: e.matmul(hp, wt[:, kc, :], xm_[:, kc, :], start=(kc == 0), stop=(kc == 7)),
                                 reads=[wt_tok, xm_tok], writes=[hp_tok])
                        if s_ == 0:
                            P.op("act", lambda e, hp=hp: e.copy(out=R32, in_=hp), reads=[hp_tok, work_t], writes=[work_t])
                        elif s_ == 1:
                            P.op("dve", lambda e, hp=hp: e.tensor_copy(out=K32, in_=hp), reads=[hp_tok, work_t], writes=[work_t])
                        else:
                            P.op("act", lambda e, hp=hp, V16=V16: e.copy(out=V16, in_=hp), reads=[hp_tok], writes=[v16_tok])
                    p_r, pr_tok, p_k, pk_tok = R32, work_t, K32, work_t
                    hp, hp_tok = half.next()
                    P.op("pe", lambda e, hp=hp, twt=twt, fs=fs, z=z: e.matmul(hp, w2c[:, fs], twt, start=True, stop=True), reads=[lwc_t, tw_tok], writes=[hp_tok])
                    P.op("act", lambda e, hp=hp, fc=fc, z=z: e.activation(out=T1, in_=hp, func=AF.Exp, bias=rwp[:, 15 + z, fc:fc + 1], scale=-1.0),
                         reads=[hp_tok, rwp_t], writes=[work_t])
                    P.op("act", lambda e: e.activation(out=T1, in_=T1, func=AF.Ln, bias=cst[:, 0:1], scale=1.0), reads=[work_t, cst_t], writes=[work_t])
                    P.op("act", lambda e: e.activation(out=EW, in_=T1, func=AF.Exp, bias=cst[:, 1:2], scale=-1.0), reads=[work_t, cst_t], writes=[work_t])
                    hp, hp_tok = half.next()
                    P.op("pe", lambda e, hp=hp, fs=fs, z=z, ta_=tat[z][0]: e.matmul(hp, a2s[z][:, fs], ta_, start=True, stop=True), reads=[lw_t, tat[z][1]], writes=[hp_tok])
                    P.op("act", lambda e, hp=hp, fc=fc, z=z: e.activation(out=AA, in_=hp, func=AF.Sigmoid, bias=rwp[:, 8 + z, fc:fc + 1], scale=1.0),
                         reads=[hp_tok, rwp_t], writes=[work_t])
                    if final:
                        hp, hp_tok = half.next()
                        P.op("pe", lambda e, hp=hp, fs=fs, ta_=tat[1][0]: e.matmul(hp, a2s[1][:, fs], ta_, start=True, stop=True), reads=[lw_t, tat[1][1]], writes=[hp_tok])
                        P.op("act", lambda e, hp=hp, fc=fc: e.activation(out=AA1, in_=hp, func=AF.Sigmoid, bias=rwp[:, 9, fc:fc + 1], scale=1.0),
                             reads=[hp_tok, rwp_t], writes=[work_t])
                    P.op("dve", lambda e, fc=fc: e.tensor_scalar(out=T2, in0=AA, scalar1=rwp[:, 11, fc:fc + 1], scalar2=rwp[:, 17, fc:fc + 1], op0=ALU.mult, op1=ALU.add),
                         reads=[work_t, rwp_t], writes=[work_t])
                    P.op("dve", lambda e, p_k=p_k: e.tensor_tensor(out=KD, in0=T2, in1=p_k, op=ALU.mult), reads=[work_t, pk_tok], writes=[work_t])
                    P.op("dve", lambda e, p_k=p_k, fc=fc: e.tensor_scalar(out=KK, in0=p_k, scalar1=rwp[:, 10, fc:fc + 1], scalar2=None, op0=ALU.mult),
                         reads=[pk_tok, rwp_t], writes=[work_t])
                    P.op("act", lambda e: e.activation(out=SQ, in_=KK, func=AF.Square), reads=[work_t], writes=[work_t])
                    hp, hp_tok = half.next()
                    P.op("pe", lambda e, hp=hp: e.matmul(hp, bdones[:], SQ, start=True, stop=True), reads=[work_t, cst_t], writes=[hp_tok])
                    P.op("dve", lambda e, hp=hp: e.tensor_scalar(out=T3, in0=hp, scalar1=1e-24, scalar2=None, op0=ALU.max), reads=[hp_tok], writes=[work_t])
                    P.op("act", lambda e: e.activation(out=T3, in_=T3, func=AF.Ln), reads=[work_t], writes=[work_t])
                    P.op("act", lambda e: e.activation(out=T3, in_=T3, func=AF.Exp, scale=-0.5), reads=[work_t], writes=[work_t])
                    P.op("dve", lambda e: e.tensor_tensor(out=KK, in0=KK, in1=T3, op=ALU.mult), reads=[work_t], writes=[work_t])
                    P.op("dve", lambda e: e.tensor_tensor(out=BE, in0=KK, in1=AA, op=ALU.mult), reads=[work_t], writes=[work_t])
                    if final:
                        P.op("dve", lambda e: e.tensor_tensor(out=T2, in0=AA, in1=AA1, op=ALU.add), reads=[work_t], writes=[work_t])
                        P.op("dve", lambda e, fc=fc: e.tensor_scalar(out=T2, in0=T2, scalar1=rwp[:, 11, fc:fc + 1], scalar2=rwp[:, 18, fc:fc + 1], op0=ALU.mult, op1=ALU.add),
                             reads=[work_t, rwp_t], writes=[work_t])
                        P.op("dve", lambda e, p_k=p_k: e.tensor_tensor(out=T2, in0=T2, in1=p_k, op=ALU.mult), reads=[work_t, pk_tok], writes=[work_t])
                        P.op("dve", lambda e, p_r=p_r, fc=fc: e.scalar_tensor_tensor(out=SQ, in0=p_r, scalar=rwp[:, 12, fc:fc + 1], in1=T2, op0=ALU.mult, op1=ALU.mult),
                             reads=[work_t, pr_tok, rwp_t], writes=[work_t])
                        hp, hp_tok = half.next()
                        P.op("pe", lambda e, hp=hp: e.matmul(hp, bdones[:], SQ, start=True, stop=True), reads=[work_t, cst_t], writes=[hp_tok])
                        P.op("dve", lambda e, hp=hp, V16=V16: e.tensor_tensor(out=BV, in0=hp, in1=V16, op=ALU.mult), reads=[hp_tok, v16_tok, fin_t], writes=[fin_t])
                        hp, hp_tok = half.next()
                        P.op("pe", lambda e, hp=hp, fs=fs: e.matmul(hp, g2s[:, fs], tgt, start=True, stop=True), reads=[lw_t, tg_tok], writes=[hp_tok])
                        P.op("act", lambda e, hp=hp: e.copy(out=G32, in_=hp), reads=[hp_tok, fin_t], writes=[fin_t])
                    if rw_stop <= 2:
                        continue
                    for ch in range(4):
                        cs = slice(ch * 64, (ch + 1) * 64)
                        P.op("dve", lambda e, cs=cs: e.tensor_tensor_scan(out=LL[:, cs], data0=onesf[:, 0:64], data1=EW[:, cs], initial=0.0, op0=ALU.mult, op1=ALU.add),
                             reads=[work_t, cst_t], writes=[work_t])
                    LL3 = LL.rearrange("p (c q) -> p c q", q=64)
                    PX3 = PX.rearrange("p (c q) -> p c q", q=64)
                    PD3 = PD.rearrange("p (c q) -> p c q", q=64)
                    P.op("dve", lambda e: e.tensor_tensor(out=PX, in0=LL, in1=EW, op=ALU.subtract), reads=[work_t], writes=[work_t])
                    P.op("dve", lambda e, LL3=LL3, PD3=PD3: e.tensor_tensor(out=PD3, in0=LL3, in1=LL3[:, :, 63:64].broadcast_to([128, 4, 64]), op=ALU.subtract),
                         reads=[work_t], writes=[work_t])
                    if z == 1:
                        P.op("dve", lambda e, LL3=LL3, PX3=PX3: e.tensor_tensor(out=PX3, in0=PX3, in1=LL3[:, :, 63:64].broadcast_to([128, 4, 64]), op=ALU.subtract),
                             reads=[work_t], writes=[work_t])
                    if z == 0:
                        especs = [(PX, -1.0), (LL, -1.0), (LL, 1.0), (PD, 1.0)]
                    else:
                        P.op("dve", lambda e: e.tensor_tensor(out=T2, in0=LL, in1=EW, op=ALU.subtract), reads=[work_t], writes=[work_t])
                        especs = [(PD, 1.0), (PX, 1.0), (PX, -1.0), (T2, -1.0)]
                    AR, ar_tok = ARr.next()
                    KT, kt_tok = KTr.next()
                    BT, bt_tok = BTr.next()
                    KH, kh_tok = KHr.next()
                    NBH, nbh_tok = NBHr.next()
                    KK3 = KK.rearrange("p (c q) -> p c q", q=64)
                    E, e_tok = EE.next()
                    P.op("act", lambda e, E=E, sp_=especs[0]: e.activation(out=E, in_=sp_[0], func=AF.Exp, scale=sp_[1]), reads=[work_t], writes=[e_tok])
                    P.op("dve", lambda e, E=E, AR=AR, KK3=KK3: e.tensor_tensor(out=AR[:, :, 0, :], in0=KK3, in1=E.rearrange("p (c q) -> p c q", q=64), op=ALU.mult),
                         reads=[work_t, e_tok], writes=[ar_tok])
                    E, e_tok = EE.next()
                    P.op("act", lambda e, E=E, sp_=especs[1]: e.activation(out=E, in_=sp_[0], func=AF.Exp, scale=sp_[1]), reads=[work_t], writes=[e_tok])
                    P.op("dve", lambda e, E=E, AR=AR, p_r=p_r: e.tensor_tensor(out=AR[:, :, 1, :], in0=p_r.rearrange("p (c q) -> p c q", q=64),
                                                                             in1=E.rearrange("p (c q) -> p c q", q=64), op=ALU.mult),
                         reads=[pr_tok, e_tok], writes=[ar_tok])
                    E, e_tok = EE.next()
                    P.op("act", lambda e, E=E, sp_=especs[2]: e.activation(out=E, in_=sp_[0], func=AF.Exp, scale=sp_[1]), reads=[work_t], writes=[e_tok])
                    P.op("dve", lambda e, E=E, KT=KT: e.tensor_tensor(out=KT, in0=KD, in1=E, op=ALU.mult), reads=[work_t, e_tok], writes=[kt_tok])
                    P.op("dve", lambda e, E=E, BT=BT: e.tensor_tensor(out=BT, in0=BE, in1=E, op=ALU.mult), reads=[work_t, e_tok], writes=[bt_tok])
                    E, e_tok = EE.next()
                    P.op("act", lambda e, E=E, sp_=especs[3]: e.activation(out=E, in_=sp_[0], func=AF.Exp, scale=sp_[1]), reads=[work_t], writes=[e_tok])
                    P.op("dve", lambda e, E=E, KH=KH: e.tensor_tensor(out=KH, in0=KD, in1=E, op=ALU.mult), reads=[work_t, e_tok], writes=[kh_tok])
                    P.op("dve", lambda e, E=E, NBH=NBH: e.scalar_tensor_tensor(out=NBH, in0=BE, scalar=-1.0, in1=E, op0=ALU.mult, op1=ALU.mult),
                         reads=[work_t, e_tok], writes=[nbh_tok])
                    if rw_stop <= 3:
                        continue
                    if dbg and seg == 0 and fc == 0 and tw == dbg_tw and z == dbg_z:
                        for di, (src_, tk_) in enumerate(((AA, work_t), (KD, work_t), (KK, work_t), (EW, work_t), (LL, work_t), (BE, work_t))):
                            P.op("sp", lambda e, di=di, src_=src_: e.dma_start(out=dbg_p[:, di, :], in_=src_), reads=[tk_], dma=True)
                    XT, xt_tok = XTr.next()
                    for kind, (src, src_tok) in enumerate(((V16, v16_tok), (KH, kh_tok), (NBH, nbh_tok))):
                        for ch in range(4):
                            P.op("pe", lambda e, ch=ch, src=src: e.matmul(b4[0:64, ch * 128:(ch + 1) * 128], src[:, ch * 64:(ch + 1) * 64], identb[:], start=True, stop=True),
                                 reads=[src_tok, ident_t], writes=[b4_t])
                        if kind % 2:
                            P.op("act", lambda e, XT=XT, kind=kind: e.copy(out=XT[:, kind, :, :], in_=b4[0:64, :].rearrange("p (c f) -> p c f", f=128)), reads=[b4_t], writes=[xt_tok])
                        else:
                            P.op("dve", lambda e, XT=XT, kind=kind: e.tensor_copy(out=XT[:, kind, :, :], in_=b4[0:64, :].rearrange("p (c f) -> p c f", f=128)), reads=[b4_t], writes=[xt_tok])
                    WCR, wcr_tok = WCRr.next()
                    for hh in range(2):
                        f0 = hh * 64
                        P.op("act", lambda e, hh=hh, f0=f0, WCR=WCR, LL3=LL3: e.activation(out=WCR[:, hh * 4:(hh + 1) * 4].unsqueeze(2), in_=LL3[f0:f0 + 64, :, 63:64],
                                                                                        func=AF.Exp, scale=-1.0),
                             reads=[work_t], writes=[wcr_tok])
                    if rw_stop <= 4:
                        continue
                    IT, it_tok = ITr.next()
                    MB, mb_tok = MBr.next()
                    PR, pr2_tok = PRr.next()
                    inst = [(hh, ch) for hh in range(2) for ch in range(4)]
                    for j, (hh, ch) in enumerate(inst):
                        f0 = hh * 64
                        bk, sl_ = j // 2, (j % 2) * 256
                        cs = slice(ch * 64, (ch + 1) * 64)
                        P.op("pe", lambda e, bk=bk, sl_=sl_, f0=f0, ch=ch, cs=cs, BT=BT, AR=AR: e.matmul(
                            b03[bk][0:64, sl_:sl_ + 128], BT[f0:f0 + 64, cs], AR[f0:f0 + 64, ch, :, :], start=True, stop=True),
                            reads=[bt_tok, ar_tok], writes=[b03_t[bk]])
                        P.op("pe", lambda e, bk=bk, sl_=sl_, f0=f0, ch=ch, cs=cs, KT=KT, AR=AR: e.matmul(
                            b03[bk][0:64, sl_ + 128:sl_ + 256], KT[f0:f0 + 64, cs], AR[f0:f0 + 64, ch, :, :], start=True, stop=True),
                            reads=[kt_tok, ar_tok], writes=[b03_t[bk]])
                    for bk in range(4):
                        P.op("dve", lambda e, bk=bk, IT=IT, mz=mz: e.tensor_tensor(
                            out=IT[:, 2 * bk:2 * bk + 2, 0:256], in0=b03[bk][0:64, :].rearrange("p (j c) -> p j c", j=2),
                            in1=mz.unsqueeze(1).broadcast_to([64, 2, 256]), op=ALU.mult),
                            reads=[b03_t[bk], masks_t], writes=[it_tok])
                    if rw_stop <= 5:
                        continue
                    for j, (hh, ch) in enumerate(inst):
                        f0 = hh * 64
                        bk, sl_ = j // 2, (j % 2) * 192
                        if rw_stop > 5.2:
                            P.op("pe", lambda e, bk=bk, sl_=sl_, j=j, IT=IT: e.matmul(
                                b03[bk][0:64, sl_:sl_ + 64], IT[:, j, 0:64], identb[0:64, 0:64], start=True, stop=True),
                                reads=[it_tok, ident_t], writes=[b03_t[bk]])
                        if rw_stop > 5.4:
                            P.op("pe", lambda e, bk=bk, sl_=sl_, j=j, f0=f0, ch=ch, IT=IT, XT=XT: e.matmul(
                                b03[bk][0:64, sl_ + 64:sl_ + 128], IT[:, j, 128:192], XT[:, 0, ch, f0:f0 + 64], start=True, stop=True),
                                reads=[it_tok, xt_tok], writes=[b03_t[bk]])
                        if rw_stop > 5.6:
                            P.op("pe", lambda e, bk=bk, sl_=sl_, f0=f0, ch=ch, AR=AR: e.matmul(
                                b03[bk][0:64, sl_ + 128:sl_ + 192], AR[:, ch, 0, :], identb[:, f0:f0 + 64], start=True, stop=True),
                                reads=[ar_tok, ident_t], writes=[b03_t[bk]])
                    if rw_stop <= 5.8:
                        continue
                    for bk in range(4):
                        P.op("act", lambda e, bk=bk, IT=IT: e.copy(
                            out=IT[:, 2 * bk:2 * bk + 2, 256:448], in_=b03[bk][0:64, 0:384].rearrange("p (j c) -> p j c", j=2)),
                            reads=[b03_t[bk]], writes=[it_tok])
                    if rw_stop <= 6:
                        continue
                    for lvl in range(6):
                        for j in range(8):
                            bk, sl_ = j // 2, (j % 2) * 192
                            P.op("pe", lambda e, bk=bk, sl_=sl_, j=j, IT=IT, MB=MB, lvl=lvl: e.matmul(
                                b03[bk][0:64, sl_:sl_ + 192], (IT[:, j, 0:64] if lvl == 0 else MB[:, j, :]), IT[:, j, 256:448], start=True, stop=True),
                                reads=[it_tok, mb_tok], writes=[b03_t[bk]])
                        if lvl < 5:
                            for j in range(8):
                                P.op("pe", lambda e, j=j, IT=IT, MB=MB, lvl=lvl: e.matmul(
                                    b4[0:64, j * 64:(j + 1) * 64], IT[:, j, 256:320], (IT[:, j, 0:64] if lvl == 0 else MB[:, j, :]), start=True, stop=True),
                                    reads=[it_tok, mb_tok], writes=[b4_t])
                        for bk in range(4):
                            bv_ = b03[bk][0:64, 0:384].rearrange("p (j c) -> p j c", j=2)
                            if lvl < 5:
                                P.op("act", lambda e, bk=bk, IT=IT, bv_=bv_: e.copy(out=IT[:, 2 * bk:2 * bk + 2, 256:320], in_=bv_[:, :, 0:64]),
                                     reads=[b03_t[bk]], writes=[it_tok])
                            P.op("dve", lambda e, bk=bk, IT=IT, bv_=bv_: e.tensor_tensor(out=IT[:, 2 * bk:2 * bk + 2, 320:448], in0=IT[:, 2 * bk:2 * bk + 2, 320:448],
                                                                                        in1=bv_[:, :, 64:192], op=ALU.add),
                                 reads=[b03_t[bk], it_tok], writes=[it_tok])
                        if lvl < 5:
                            P.op("act", lambda e, MB=MB: e.copy(out=MB, in_=b4[0:64, :].rearrange("p (j c) -> p j c", j=8)), reads=[b4_t], writes=[mb_tok])
                    if rw_stop <= 7:
                        continue
                    for j, (hh, ch) in enumerate(inst):
                        f0 = hh * 64
                        bk, sl_ = j // 2, (j % 2) * 128
                        P.op("pe", lambda e, bk=bk, sl_=sl_, j=j, f0=f0, ch=ch, IT=IT, XT=XT: e.matmul(
                            b03[bk][0:64, sl_:sl_ + 64], IT[:, j, 384:448], XT[:, 2, ch, f0:f0 + 64], start=True, stop=True),
                            reads=[it_tok, xt_tok], writes=[b03_t[bk]])
                        P.op("pe", lambda e, bk=bk, sl_=sl_, f0=f0, ch=ch, AR=AR: e.matmul(
                            b03[bk][0:64, sl_ + 64:sl_ + 128], identb[:, f0:f0 + 64], AR[:, ch, 1, :], start=True, stop=False),
                            reads=[ar_tok, ident_t], writes=[b03_t[bk]])
                        P.op("pe", lambda e, bk=bk, sl_=sl_, j=j, IT=IT: e.matmul(
                            b03[bk][0:64, sl_ + 64:sl_ + 128], IT[:, j, 384:448], IT[:, j, 64:128], start=False, stop=True),
                            reads=[it_tok], writes=[b03_t[bk]])
                    for bk in range(4):
                        P.op("act", lambda e, bk=bk, PR=PR: e.copy(out=PR[:, 2 * bk:2 * bk + 2, :], in_=b03[bk][0:64, 0:256].rearrange("p (j c) -> p j c", j=2)),
                             reads=[b03_t[bk]], writes=[pr2_tok])
                    if rw_stop <= 8:
                        continue
                    order = [0, 1, 2, 3] if z == 0 else [3, 2, 1, 0]
                    st_tok = S_t[fc]
                    for ch in order:
                        cs = slice(ch * 64, (ch + 1) * 64)
                        for hh in range(2):
                            j = hh * 4 + ch
                            f0 = hh * 64
                            hd = fc * 2 + hh
                            bk = hh + 2 * (ch % 2)
                            P.op("pe", lambda e, bk=bk, j=j, f0=f0, ch=ch, IT=IT, XT=XT: e.matmul(
                                b03[bk][0:64, 0:64], XT[:, 0, ch, f0:f0 + 64], IT[:, j, 192:256], start=True, stop=False),
                                reads=[xt_tok, it_tok], writes=[b03_t[bk]])
                            P.op("pe", lambda e, bk=bk, j=j, IT=IT: e.matmul(
                                b03[bk][0:64, 0:64], IT[:, j, 320:384], IT[:, j, 64:128], start=False, stop=False),
                                reads=[it_tok], writes=[b03_t[bk]])
                            P.op("pe", lambda e, bk=bk, j=j, hd=hd, PR=PR: e.matmul(
                                b03[bk][0:64, 0:64], Sb[:, hd, :], PR[:, j, 64:128], start=False, stop=True),
                                reads=[st_tok, pr2_tok], writes=[b03_t[bk]])
                            P.op("pe", lambda e, bk=bk, j=j, f0=f0, ch=ch, IT=IT, XT=XT: e.matmul(
                                b03[bk][0:64, 64:128], XT[:, 2, ch, f0:f0 + 64], IT[:, j, 320:384], start=True, stop=False),
                                reads=[xt_tok, it_tok], writes=[b03_t[bk]])
                            P.op("pe", lambda e, bk=bk, f0=f0, ch=ch, XT=XT: e.matmul(
                                b03[bk][0:64, 64:128], XT[:, 1, ch, f0:f0 + 64], XT[:, 0, ch, f0:f0 + 64], start=False, stop=False),
                                reads=[xt_tok], writes=[b03_t[bk]])
                            P.op("pe", lambda e, bk=bk, j=j, hd=hd, PR=PR: e.matmul(
                                b03[bk][0:64, 64:128], PR[:, j, 0:64], Sb[:, hd, :], start=False, stop=True),
                                reads=[st_tok, pr2_tok], writes=[b03_t[bk]])
                            if is_smp:
                                P.op("pe", lambda e, bk=bk, j=j, hd=hd, PR=PR: e.matmul(
                                    b03[bk][0:64, 128:192], PR[:, j, 0:64], Fb[:, hd, :], start=True, stop=True),
                                    reads=[st_tok, pr2_tok], writes=[b03_t[bk]])
                        for hh in range(2):
                            j = hh * 4 + ch
                            f0 = hh * 64
                            hd = fc * 2 + hh
                            bk = hh + 2 * (ch % 2)
                            tsl = slice(t0 + ch * 64, t0 + (ch + 1) * 64)
                            if not final:
                                P.op("act", lambda e, bk=bk, f0=f0, fc=fc, tsl=tsl: e.copy(out=Yb[f0:f0 + 64, fc, tsl], in_=b03[bk][0:64, 0:64]),
                                     reads=[b03_t[bk]], writes=[Yb_t[tw]])
                            else:
                                P.op("dve", lambda e, bk=bk, f0=f0, fc=fc, tsl=tsl, cs=cs: e.tensor_tensor(out=YS[f0:f0 + 64, cs], in0=b03[bk][0:64, 0:64],
                                                                                                         in1=Yb[f0:f0 + 64, fc, tsl], op=ALU.add),
                                     reads=[b03_t[bk], Yb_t[tw], fin_t], writes=[fin_t])
                            P.op("dve", lambda e, bk=bk, hd=hd, j=j, WCR=WCR: e.scalar_tensor_tensor(out=S32[:, hd, :], in0=S32[:, hd, :], scalar=WCR[:, j:j + 1],
                                                                                                  in1=b03[bk][0:64, 64:128], op0=ALU.mult, op1=ALU.add),
                                 reads=[b03_t[bk], wcr_tok, st_tok], writes=[st_tok])
                            if is_smp:
                                P.op("dve", lambda e, bk=bk, hd=hd, j=j, WCR=WCR: e.scalar_tensor_tensor(out=F32s[:, hd, :], in0=F32s[:, hd, :], scalar=WCR[:, j:j + 1],
                                                                                                      in1=b03[bk][0:64, 128:192], op0=ALU.mult, op1=ALU.add),
                                     reads=[b03_t[bk], wcr_tok, st_tok], writes=[st_tok])
                        P.op("act", lambda e, fc=fc: e.copy(out=Sb[:, 2 * fc:2 * fc + 2, :], in_=S32[:, 2 * fc:2 * fc + 2, :]), reads=[st_tok], writes=[st_tok])
                        if is_smp:
                            P.op("act", lambda e, fc=fc: e.copy(out=Fb[:, 2 * fc:2 * fc + 2, :], in_=F32s[:, 2 * fc:2 * fc + 2, :]), reads=[st_tok], writes=[st_tok])
                    if rw_stop <= 9:
                        continue
                    if final and dbg and seg == 0:
                        P.op("sp", lambda e, fc=fc, t0=t0: e.dma_start(out=dbg_ys[:, fc, t0:t0 + TW], in_=YS), reads=[fin_t], dma=True)
                    if final:
                        P.op("act", lambda e: e.copy(out=SQ, in_=YS), reads=[fin_t, work_t], writes=[work_t])
                        hp, hp_tok = half.next()
                        P.op("pe", lambda e, hp=hp: e.matmul(hp, bdones[:], SQ, start=True, stop=True), reads=[work_t, cst_t], writes=[hp_tok])
                        P.op("dve", lambda e, hp=hp: e.scalar_tensor_tensor(out=YC, in0=hp, scalar=-1.0 / 64, in1=YS, op0=ALU.mult, op1=ALU.add),
                             reads=[hp_tok, fin_t], writes=[fin_t])
                        P.op("act", lambda e: e.activation(out=SQ, in_=YC, func=AF.Square), reads=[fin_t, work_t], writes=[work_t])
                        hp, hp_tok = half.next()
                        P.op("pe", lambda e, hp=hp: e.matmul(hp, bdones[:], SQ, start=True, stop=True), reads=[work_t, cst_t], writes=[hp_tok])
                        P.op("act", lambda e, hp=hp: e.activation(out=T3, in_=hp, func=AF.Ln, bias=cst[:, 2:3], scale=1.0 / 64), reads=[hp_tok, cst_t, work_t], writes=[work_t])
                        P.op("act", lambda e: e.activation(out=T3, in_=T3, func=AF.Exp, scale=-0.5), reads=[work_t], writes=[work_t])
                        P.op("dve", lambda e: e.tensor_tensor(out=YC, in0=YC, in1=T3, op=ALU.mult), reads=[work_t, fin_t], writes=[fin_t])
                        P.op("dve", lambda e, fc=fc: e.tensor_scalar(out=YC, in0=YC, scalar1=rwp[:, 13, fc:fc + 1], scalar2=rwp[:, 14, fc:fc + 1], op0=ALU.mult, op1=ALU.add),
                             reads=[fin_t, rwp_t], writes=[fin_t])
                        P.op("dve", lambda e: e.tensor_tensor(out=YC, in0=YC, in1=BV, op=ALU.add), reads=[fin_t], writes=[fin_t])
                        P.op("dve", lambda e, fc=fc, ob=ob: e.tensor_tensor(out=ob[:, fc, :], in0=YC, in1=G32, op=ALU.mult), reads=[fin_t], writes=[ob_tok])
                if final and ti + 1 < len(tiles):
                    pre_hx = hx_tile(tiles[ti + 1])
                if final and rw_stop > 10:
                    for d in range(8):
                        wb, wb_tok = wro.next()
                        P.op("sp", lambda e, d=d, wb=wb: e.dma_start(out=wb, in_=wro_s[d]), reads=[wro_t[d]], writes=[wb_tok], dma=True)
                        hp, hp_tok = half.next()
                        for kc in range(8):
                            P.op("pe", lambda e, kc=kc, wb=wb, hp=hp, ob=ob: e.matmul(hp, wb[:, kc, :], ob[:, kc, :], start=(kc == 0), stop=(kc == 7)),
                                 reads=[wb_tok, ob_tok], writes=[hp_tok])
                        P.op("dve", lambda e, d=d, hp=hp, t0=t0: e.tensor_tensor(out=H[:, d, t0:t0 + TW], in0=H[:, d, t0:t0 + TW], in1=hp, op=ALU.add),
                             reads=[hp_tok, H_tok[d][tt]], writes=[H_tok[d][tt]])

    def load_seg(seg):
        P.barrier()
        cv = Carver()
        for j_ in range(2):
            P.op("sp", lambda e, j_=j_: e.dma_start(out=Hh[:, j_, :], in_=xh_d[seg, j_].rearrange("(c p) -> p c", p=128), allow_slow_non_contiguous=True),
                 writes=[Hh_t], dma=True)
        xstage = cv.ring(2, [128, 4, D], F32)
        pbank = Ring([bank(k) for k in range(8)])
        for tt in range(ntt):
            g0 = seg * T + tt * TT
            xs, xs_tok = xstage.next()
            P.op("sp", lambda e, xs=xs, g0=g0: e.dma_start(out=xs, in_=x_d[g0:g0 + TT, :].rearrange("(k q) f -> q k f", q=128)),
                 writes=[xs_tok], dma=True)
            for c in range(8):
                pp, pp_tok = pbank.next()
                for k in range(4):
                    P.op("pe", lambda e, c=c, k=k, pp=pp, xs=xs: e.transpose(pp[:, k * 128:(k + 1) * 128],
                                                                          xs[:, k, c * 128:(c + 1) * 128], ident[:]),
                         reads=[xs_tok, ident_t], writes=[pp_tok])
                if c % 2:
                    P.op("act", lambda e, c=c, pp=pp, tt=tt: e.copy(out=H[:, c, tt * TT:(tt + 1) * TT], in_=pp),
                         reads=[pp_tok], writes=[H_tok[c][tt]])
                else:
                    P.op("dve", lambda e, c=c, pp=pp, tt=tt: e.tensor_copy(out=H[:, c, tt * TT:(tt + 1) * TT], in_=pp),
                         reads=[pp_tok], writes=[H_tok[c][tt]])

    def store_seg(seg):
        P.barrier()
        cv = Carver()
        nb = NormBufs(cv, 7)
        ostage = cv.ring(3, [128, D], F32)
        pbank = Ring([bank(k) for k in range(7)])
        for tt in range(ntt):
            t0 = tt * TT
            if do_final:
                fin_tok = Tok()
                rmsnorm_tile(nb, tt, 6, lambda c, t0=t0: H[:, c, t0:t0 + TT], fin_tok)
                rd = [fin_tok]
            else:
                rd = []
            for k in range(4):
                ob, ob_tok = ostage.next()
                for half in range(2):
                    pp, pp_tok = pbank.next()
                    for cc in range(4):
                        c = half * 4 + cc
                        P.op("pe", lambda e, c=c, cc=cc, pp=pp, k=k, t0=t0: e.transpose(
                            pp[:, cc * 128:(cc + 1) * 128], H[:, c, t0 + k * 128:t0 + (k + 1) * 128], ident[:]),
                            reads=[H_tok[c][tt], ident_t] + rd, writes=[pp_tok])
                    if half:
                        P.op("act", lambda e, pp=pp, ob=ob: e.copy(out=ob[:, 512:1024], in_=pp), reads=[pp_tok], writes=[ob_tok])
                    else:
                        P.op("dve", lambda e, pp=pp, ob=ob: e.tensor_copy(out=ob[:, 0:512], in_=pp), reads=[pp_tok], writes=[ob_tok])
                g0 = seg * T + t0 + k * 128
                P.op("sp", lambda e, ob=ob, g0=g0: e.dma_start(out=y_d[g0:g0 + 128, :], in_=ob),
                     reads=[ob_tok], dma=True)

    for seg in range(nseg):
        load_seg(seg)
        for i in range(DEPTH):
            if do_ffn:
                ffn_phase(i, 0, halo=(i == 0))
            if i == 0 and do_rwkv and rw_stop > -2:
                rwkv_phase(seg)
            if i == 1 and do_na:
                na_phase(seg)
            if do_ffn:
                ffn_phase(i, 1)
            if do_ple:
                ple_phase(i, seg)
        store_seg(seg)

    P.emit()
    es.close()
    return nc


_NC_CACHE = {}
W_KEYS = ("ffn_norm", "ffn_w_in", "ffn_w_out", "mix_norm", "ple_norm", "ple_w_gate", "ple_w_proj", "final_norm",
          "na_w_qkv", "na_b_qkv", "na_w_o", "na_b_o", "rw_mu", "rw_w_rkv", "rw_w0", "rw_w1", "rw_w2", "rw_a0", "rw_a1", "rw_a2",
          "rw_g1", "rw_g2", "rw_k_k", "rw_k_a", "rw_r_k", "rw_lnx_w", "rw_lnx_b", "rw_w_o")


def make_masks():
    idx = np.arange(64)
    out = np.zeros((2, 64, 256), np.float32)
    for z in range(2):
        if z == 0:
            su = (idx[:, None] < idx[None, :]).astype(np.float32)
            u = (idx[:, None] <= idx[None, :]).astype(np.float32)
        else:
            su = (idx[:, None] > idx[None, :]).astype(np.float32)
            u = (idx[:, None] >= idx[None, :]).astype(np.float32)
        out[z] = np.concatenate([-su, -u, su, u], axis=1)
    return out


def make_btab(rpb):
    q = np.arange(64)[:, None]
    k = np.arange(64)[None, :]
    ws = np.clip(q - 8, 0, 48)
    mask = (k >= ws) & (k < ws + 16)
    cidx = np.clip(k - q, -15, 15) + 15
    tab = rpb[:, :, cidx]
    tab = np.where(mask[None, None], tab, np.float32(-1e30))
    return np.ascontiguousarray(np.transpose(tab, (0, 2, 1, 3)).reshape(NA_HEADS, 64, 15 * 64).astype(np.float32))


def run_cores(nseg, T, x_cores, p_cores, weights, xh_cores=None, extra=None, **kw):
    key = (nseg, T, tuple(sorted(kw.items())))
    if key not in _NC_CACHE:
        _NC_CACHE[key] = build(nseg, T, **kw)
    nc = _NC_CACHE[key]
    ident = np.eye(128, dtype=np.float32)
    btab = make_btab(weights["na_rpb"][0])
    masks = make_masks()
    bdones = np.kron(np.eye(2, dtype=np.float32), np.ones((64, 64), np.float32))
    in_maps = []
    for c in range(len(x_cores)):
        m = {"x": x_cores[c], "p": p_cores[c], "ident": ident, "btab": btab, "masks": masks, "bdones": bdones}
        m["xh"] = xh_cores[c] if xh_cores is not None else np.zeros((nseg, 2, D), np.float32)
        for k in W_KEYS:
            m[k] = weights[k]
        if extra is not None:
            m.update(extra[c])
        in_maps.append(m)
    res = run_bass_kernel_spmd(nc, in_maps, core_ids=list(range(len(x_cores))))
    if kw.get("dbg") or kw.get("sample"):
        return [r for r in res.results]
    return [r["y"] for r in res.results]


def sample_rounds(nseg, T, ncore, x_cores, p_cores, xh_cores, w, **kw):
    z32 = lambda *sh: np.zeros(sh, np.float32)
    extra = [dict(summ_all=z32(8, 2, 2, 64, D), cmask=z32(64, 16), kvh_k=z32(2, D, 256), kvh_v=z32(2, 256, D),
                  sel=np.tile(np.array([[1, 0, 1, 0]], np.float32), (64, 1))) for _ in range(ncore)]
    res = run_cores(nseg, T, x_cores, p_cores, w, xh_cores, extra, sample=True, **kw)
    summ_all = z32(8, 2, 2, 64, D)
    for j in range(ncore):
        so = res[j]["summ_o"]
        for z in range(2):
            summ_all[j, z, 0] = so[z, 0]
            summ_all[j, z, 1] = so[z, 1].reshape(64, 16, 64).transpose(2, 1, 0).reshape(64, D)
    for c in range(ncore):
        cm = z32(64, 16)
        for j in range(ncore):
            if j < c:
                cm[:, j] = 1.0
            if j > c:
                cm[:, 8 + j] = 1.0
        extra[c]["summ_all"] = summ_all
        extra[c]["cmask"] = cm
    res = run_cores(nseg, T, x_cores, p_cores, w, xh_cores, extra, sample=True, **kw)
    for c in range(ncore):
        kk, vv = z32(2, D, 256), z32(2, 256, D)
        if c > 0:
            kk[0], vv[0] = res[c - 1]["kvo_k"][1], res[c - 1]["kvo_v"][1]
        if c < ncore - 1:
            kk[1], vv[1] = res[c + 1]["kvo_k"][0], res[c + 1]["kvo_v"][0]
        extra[c]["kvh_k"], extra[c]["kvh_v"] = kk, vv
        ta, ba = float(c == 0), float(c == ncore - 1)
        extra[c]["sel"] = np.tile(np.array([[ta, 1 - ta, ba, 1 - ba]], np.float32), (64, 1))
    res = run_cores(nseg, T, x_cores, p_cores, w, xh_cores, extra, sample=True, **kw)
    return [r["y"] for r in res]


def kernel(x_prompt, x_sample, p_prompt, p_sample, **w):
    w = {k: np.ascontiguousarray(np.asarray(v, dtype=np.float32)) for k, v in w.items()}
    x_prompt = np.asarray(x_prompt, dtype=np.float32)
    x_sample = np.asarray(x_sample, dtype=np.float32)
    p_prompt = np.asarray(p_prompt, dtype=np.float32)
    p_sample = np.asarray(p_sample, dtype=np.float32)
    T = 2048
    x_cores, p_cores, xh_cores = [], [], []
    for c in range(NCORES):
        xs = [x_prompt[2 * c], x_prompt[2 * c + 1], x_sample[0, c * T:(c + 1) * T]]
        ps_ = [p_prompt[:, 2 * c], p_prompt[:, 2 * c + 1], p_sample[:, 0, c * T:(c + 1) * T]]
        x_cores.append(np.ascontiguousarray(np.concatenate(xs, axis=0)))
        p_cores.append(np.ascontiguousarray(np.concatenate(ps_, axis=1)))
        xh = np.zeros((3, 2, D), np.float32)
        if c > 0:
            xh[2, 0] = x_sample[0, c * T - 1]
        if c < NCORES - 1:
            xh[2, 1] = x_sample[0, (c + 1) * T]
        xh_cores.append(xh)
    ys = sample_rounds(3, T, NCORES, x_cores, p_cores, xh_cores, w)
    y_prompt = np.empty_like(x_prompt)
    y_sample = np.empty_like(x_sample)
    for c in range(NCORES):
        y_prompt[2 * c] = ys[c][0:T]
        y_prompt[2 * c + 1] = ys[c][T:2 * T]
        y_sample[0, c * T:(c + 1) * T] = ys[c][2 * T:3 * T]
    return (y_prompt, y_sample)
```
